# Optimizing a Trainium2 kernel written in Bass

```python
import math
import jax
import jax.numpy as jnp
from jax import lax
import numpy as np

D_MODEL = 1024
BATCH = 4
SEQ = 8192
DEPTH = 2

CHUNK = 64
Q_BLOCK = 128
H_A = 8
Q_LORA = 256
KV_LORA = 128
NOPE_DIM = 64
ROPE_DIM = 32
V_DIM_A = 64
ROPE_BASE = 10000.0
H_B = 4
DH_B = 64
T5_BUCKETS = 32
T5_MAX_DIST = 128
H_C = 8
DH_C = 64
H_D = 8
DH_D = 64
BAND_CHUNKS = 8
REL_CLIP = 128
D_FF = 2816
CONV_W = 3
EPS_LN = 1e-5
EPS_RMS = 1e-6
DEEPNORM_ALPHA = (2 * DEPTH) ** 0.25
DEEPNORM_BETA = (8 * DEPTH) ** -0.25
NEG_INF = -1e30

kernel_name = 'hybrid_mla_diff_fox_chunk_convffn'


def layer_norm(x, g, b):
    xf = x.astype(jnp.float32)
    mu = jnp.mean(xf, axis=-1, keepdims=True)
    var = jnp.mean(jnp.square(xf - mu), axis=-1, keepdims=True)
    return ((xf - mu) * lax.rsqrt(var + EPS_LN) * g + b).astype(x.dtype)


def rms_norm(x, g):
    xf = x.astype(jnp.float32)
    ms = jnp.mean(jnp.square(xf), axis=-1, keepdims=True)
    return (xf * lax.rsqrt(ms + EPS_RMS) * g).astype(x.dtype)


def split_cols(h, sizes):
    return jnp.split(h, np.cumsum(sizes)[:-1].tolist(), axis=-1)


def rope_cos_sin(seq_len):
    half = ROPE_DIM // 2
    inv = jnp.power(ROPE_BASE, -jnp.arange(half, dtype=jnp.float32) / half)
    ang = jnp.arange(seq_len, dtype=jnp.float32)[:, None] * inv[None, :]
    return jnp.cos(ang), jnp.sin(ang)


def apply_rope(x, cos, sin):
    half = x.shape[-1] // 2
    x1, x2 = x[..., :half], x[..., half:]
    cos, sin = cos.astype(x.dtype), sin.astype(x.dtype)
    return jnp.concatenate([x1 * cos - x2 * sin, x1 * sin + x2 * cos], axis=-1)


def t5_bucket(rel):
    half = T5_BUCKETS // 2
    max_exact = half // 2
    n = jnp.abs(rel)
    large = max_exact + (jnp.log(jnp.maximum(n, 1).astype(jnp.float32) / max_exact)
                         / math.log(T5_MAX_DIST / max_exact) * (half - max_exact)).astype(jnp.int32)
    large = jnp.minimum(large, half - 1)
    return jnp.where(rel > 0, half, 0) + jnp.where(n < max_exact, n, large)


def chunk_causal_mask(q0, key_idx):
    q_idx = q0 + jnp.arange(Q_BLOCK)
    return (key_idx[None, :] // CHUNK) <= (q_idx[:, None] // CHUNK)


def sweep_blocks(block_fn, n_blocks):
    out = lax.map(block_fn, jnp.arange(n_blocks))
    nb, b, l, h, d = out.shape
    return jnp.moveaxis(out, 0, 1).reshape(b, nb * l, h, d)


def modulate(x, cond, w, b):
    shift, scale, gate = jnp.split(cond @ w + b, 3, axis=-1)
    return x * (1.0 + scale[:, None, :]) + shift[:, None, :], (1.0 + gate)[:, None, :]


def mixer_ab(u, w_in, q_norm, w_uq, kv_norm, w_ukv, lq1, lk1, lq2, lk2, sub_g, t5_table, w_out, layer_idx):
    b, s_len, _ = u.shape
    wb = H_B * 2 * DH_B
    cq, ckv, kr, qb, kb, vb = split_cols(u @ w_in, (Q_LORA, KV_LORA, ROPE_DIM, wb, wb, wb))
    key_idx = jnp.arange(s_len)
    n_blocks = s_len // Q_BLOCK

    cos, sin = rope_cos_sin(s_len)
    qa = (rms_norm(cq, q_norm) @ w_uq).reshape(b, s_len, H_A, NOPE_DIM + ROPE_DIM)
    q_nope = qa[..., :NOPE_DIM]
    q_rope = apply_rope(qa[..., NOPE_DIM:], cos[:, None, :], sin[:, None, :])
    kv = (rms_norm(ckv, kv_norm) @ w_ukv).reshape(b, s_len, H_A, NOPE_DIM + V_DIM_A)
    k_nope, v_a = kv[..., :NOPE_DIM], kv[..., NOPE_DIM:]
    k_rope = apply_rope(kr, cos, sin)
    scale_a = (NOPE_DIM + ROPE_DIM) ** -0.5

    def mla_block(i):
        q0 = i * Q_BLOCK
        qn = lax.dynamic_slice_in_dim(q_nope, q0, Q_BLOCK, axis=1)
        qr = lax.dynamic_slice_in_dim(q_rope, q0, Q_BLOCK, axis=1)
        logits = (jnp.einsum('bqhd,bkhd->bhqk', qn, k_nope, preferred_element_type=jnp.float32)
                  + jnp.einsum('bqhr,bkr->bhqk', qr, k_rope, preferred_element_type=jnp.float32)) * scale_a
        logits = jnp.where(chunk_causal_mask(q0, key_idx), logits, NEG_INF)
        p = jax.nn.softmax(logits, axis=-1).astype(v_a.dtype)
        return jnp.einsum('bhqk,bkhd->bqhd', p, v_a)

    o_a = sweep_blocks(mla_block, n_blocks)

    qd = qb.reshape(b, s_len, H_B, 2, DH_B)
    kd = kb.reshape(b, s_len, H_B, 2, DH_B)
    vd = vb.reshape(b, s_len, H_B, 2 * DH_B)
    lam_init = 0.8 - 0.6 * math.exp(-0.3 * layer_idx)
    lam = (jnp.exp(jnp.sum(lq1.astype(jnp.float32) * lk1.astype(jnp.float32)))
           - jnp.exp(jnp.sum(lq2.astype(jnp.float32) * lk2.astype(jnp.float32))) + lam_init)
    scale_b = DH_B ** -0.5

    def diff_block(i):
        q0 = i * Q_BLOCK
        qq = lax.dynamic_slice_in_dim(qd, q0, Q_BLOCK, axis=1)
        q_idx = q0 + jnp.arange(Q_BLOCK)
        bias = jnp.take(t5_table, t5_bucket(key_idx[None, :] - q_idx[:, None]), axis=1)
        logits = jnp.einsum('bqhcd,bkhcd->bchqk', qq, kd, preferred_element_type=jnp.float32) * scale_b + bias
        logits = jnp.where(chunk_causal_mask(q0, key_idx), logits, NEG_INF)
        p = jax.nn.softmax(logits, axis=-1)
        attn = (p[:, 0] - lam * p[:, 1]).astype(vd.dtype)
        return jnp.einsum('bhqk,bkhd->bqhd', attn, vd)

    o_b = sweep_blocks(diff_block, n_blocks)
    o_b = rms_norm(o_b, sub_g) * (1.0 - lam_init)

    merged = jnp.concatenate([o_a.reshape(b, s_len, H_A * V_DIM_A), o_b.reshape(b, s_len, wb)], axis=-1)
    return merged @ w_out


def mixer_cd(u, w_in, b_f, rel_table, w_out):
    b, s_len, _ = u.shape
    wc, wd = H_C * DH_C, H_D * DH_D
    qc, kc, vc, fl, qd, kd, vd = split_cols(u @ w_in, (wc, wc, wc, H_C, wd, wd, wd))
    qc, kc, vc = (t.reshape(b, s_len, H_C, DH_C) for t in (qc, kc, vc))
    qd, kd, vd = (t.reshape(b, s_len, H_D, DH_D) for t in (qd, kd, vd))
    key_idx = jnp.arange(s_len)

    log_f = jax.nn.log_sigmoid(fl.astype(jnp.float32) + b_f)
    cum_f = jnp.moveaxis(jnp.cumsum(log_f, axis=1), 1, 2)
    scale_c = DH_C ** -0.5

    def fox_block(i):
        q0 = i * Q_BLOCK
        qq = lax.dynamic_slice_in_dim(qc, q0, Q_BLOCK, axis=1)
        fq = lax.dynamic_slice_in_dim(cum_f, q0, Q_BLOCK, axis=2)
        q_idx = q0 + jnp.arange(Q_BLOCK)
        logits = (jnp.einsum('bqhd,bkhd->bhqk', qq, kc, preferred_element_type=jnp.float32) * scale_c
                  + fq[..., :, None] - cum_f[:, :, None, :])
        logits = jnp.where(key_idx[None, :] <= q_idx[:, None], logits, NEG_INF)
        p = jax.nn.softmax(logits, axis=-1).astype(vc.dtype)
        return jnp.einsum('bhqk,bkhd->bqhd', p, vc)

    o_c = sweep_blocks(fox_block, s_len // Q_BLOCK)

    lead = BAND_CHUNKS * CHUNK
    band = lead + CHUNK
    pad = ((0, 0), (lead, 0), (0, 0), (0, 0))
    kp, vp = jnp.pad(kd, pad), jnp.pad(vd, pad)
    kj = jnp.arange(band)
    rel = (lead + jnp.arange(CHUNK))[:, None] - kj[None, :]
    bias_d = jnp.take(rel_table, jnp.clip(rel, -REL_CLIP, REL_CLIP) + REL_CLIP, axis=1)
    scale_d = DH_D ** -0.5

    def chunk_block(ci):
        q0 = ci * CHUNK
        qq = lax.dynamic_slice_in_dim(qd, q0, CHUNK, axis=1)
        kk = lax.dynamic_slice_in_dim(kp, q0, band, axis=1)
        vv = lax.dynamic_slice_in_dim(vp, q0, band, axis=1)
        logits = jnp.einsum('bqhd,bkhd->bhqk', qq, kk, preferred_element_type=jnp.float32) * scale_d + bias_d
        logits = jnp.where((q0 - lead + kj) >= 0, logits, NEG_INF)
        p = jax.nn.softmax(logits, axis=-1).astype(vv.dtype)
        return jnp.einsum('bhqk,bkhd->bqhd', p, vv)

    o_d = sweep_blocks(chunk_block, s_len // CHUNK)

    merged = jnp.concatenate([o_c.reshape(b, s_len, wc), o_d.reshape(b, s_len, wd)], axis=-1)
    return merged @ w_out


def conv_ffn(u, w_gate, w_val, conv_w, conv_b, w_down):
    s_len = u.shape[1]
    g = u @ w_gate
    gp = jnp.pad(g, ((0, 0), (CONV_W - 1, 0), (0, 0)))
    g = sum(conv_w[j] * gp[:, j:j + s_len] for j in range(CONV_W)) + conv_b
    return (jax.nn.silu(g) * (u @ w_val)) @ w_down


def setup_inputs(seed: int = 0) -> dict:
    key = jax.random.key(seed)
    ks = iter(jax.random.split(key, 32))

    def nrm(shape, scale):
        return jax.random.normal(next(ks), shape, jnp.float32) * scale

    d = D_MODEL
    ne, no = (DEPTH + 1) // 2, DEPTH // 2
    in_ab = Q_LORA + KV_LORA + ROPE_DIM + 3 * H_B * 2 * DH_B
    in_cd = 3 * H_C * DH_C + H_C + 3 * H_D * DH_D
    mix_ab = H_A * V_DIM_A + H_B * 2 * DH_B
    mix_cd = H_C * DH_C + H_D * DH_D
    return {
        'x': nrm((BATCH, SEQ, d), 1.0),
        'c': nrm((BATCH, d), 1.0),
        'ada_w': nrm((DEPTH, 2, d, 3 * d), 0.1 * d ** -0.5),
        'ada_b': nrm((DEPTH, 2, 3 * d), 0.01),
        'ln_g': 1.0 + nrm((DEPTH, 2, d), 0.02),
        'ln_b': nrm((DEPTH, 2, d), 0.02),
        't5_table': nrm((H_B, T5_BUCKETS), 0.5),
        'ab_w_in': nrm((ne, d, in_ab), d ** -0.5),
        'mla_q_norm': 1.0 + nrm((ne, Q_LORA), 0.02),
        'mla_w_uq': nrm((ne, Q_LORA, H_A * (NOPE_DIM + ROPE_DIM)), Q_LORA ** -0.5),
        'mla_kv_norm': 1.0 + nrm((ne, KV_LORA), 0.02),
        'mla_w_ukv': nrm((ne, KV_LORA, H_A * (NOPE_DIM + V_DIM_A)), KV_LORA ** -0.5),
        'diff_lq1': nrm((ne, DH_B), 0.1),
        'diff_lk1': nrm((ne, DH_B), 0.1),
        'diff_lq2': nrm((ne, DH_B), 0.1),
        'diff_lk2': nrm((ne, DH_B), 0.1),
        'diff_sub_g': 1.0 + nrm((ne, 2 * DH_B), 0.02),
        'ab_w_out': nrm((ne, mix_ab, d), mix_ab ** -0.5 * DEEPNORM_BETA),
        'cd_w_in': nrm((no, d, in_cd), d ** -0.5),
        'fox_b_f': 3.0 + nrm((no, H_C), 0.5),
        'chunk_rel_table': nrm((no, H_D, 2 * REL_CLIP + 1), 0.5),
        'cd_w_out': nrm((no, mix_cd, d), mix_cd ** -0.5 * DEEPNORM_BETA),
        'ffn_w_gate': nrm((DEPTH, d, D_FF), d ** -0.5),
        'ffn_w_val': nrm((DEPTH, d, D_FF), d ** -0.5),
        'ffn_conv_w': nrm((DEPTH, CONV_W, D_FF), CONV_W ** -0.5),
        'ffn_conv_b': nrm((DEPTH, D_FF), 0.02),
        'ffn_w_down': nrm((DEPTH, D_FF, d), D_FF ** -0.5 * DEEPNORM_BETA),
    }


def reference(x, c, ada_w, ada_b, ln_g, ln_b, t5_table, ab_w_in, mla_q_norm, mla_w_uq, mla_kv_norm,
              mla_w_ukv, diff_lq1, diff_lk1, diff_lq2, diff_lk2, diff_sub_g, ab_w_out, cd_w_in, fox_b_f,
              chunk_rel_table, cd_w_out, ffn_w_gate, ffn_w_val, ffn_conv_w, ffn_conv_b, ffn_w_down):
    cond = jax.nn.silu(c)
    for i in range(DEPTH):
        u, gate = modulate(x, cond, ada_w[i, 0], ada_b[i, 0])
        if i % 2 == 0:
            e = i // 2
            y = mixer_ab(u, ab_w_in[e], mla_q_norm[e], mla_w_uq[e], mla_kv_norm[e], mla_w_ukv[e],
                         diff_lq1[e], diff_lk1[e], diff_lq2[e], diff_lk2[e], diff_sub_g[e], t5_table,
                         ab_w_out[e], i)
        else:
            o = i // 2
            y = mixer_cd(u, cd_w_in[o], fox_b_f[o], chunk_rel_table[o], cd_w_out[o])
        x = layer_norm(DEEPNORM_ALPHA * x + gate * y, ln_g[i, 0], ln_b[i, 0])
        u, gate = modulate(x, cond, ada_w[i, 1], ada_b[i, 1])
        y = conv_ffn(u, ffn_w_gate[i], ffn_w_val[i], ffn_conv_w[i], ffn_conv_b[i], ffn_w_down[i])
        x = layer_norm(DEEPNORM_ALPHA * x + gate * y, ln_g[i, 1], ln_b[i, 1])
    return x
```

```python
import contextlib
import math
import numpy as np
import concourse.bass as bass
import concourse.mybir as mybir
from concourse.bass_utils import run_bass_kernel_spmd

F32 = mybir.dt.float32
BF16 = mybir.dt.bfloat16
AF = mybir.ActivationFunctionType
ALU = mybir.AluOpType

PS = 1
NCORES = 4 * PS
S = 8192
D = 1024
GW = 512
NG = S // GW
OWN = S // PS
NGO = OWN // GW
HA, HB, HC, HD = 8 // PS, 4 // PS, 8 // PS, 8 // PS
DFF = 2816
NFC = DFF // 128
ALPHA = 4 ** 0.25
EPS_LN = 1e-5
EPS_RMS = 1e-6
SCALE_A = 96 ** -0.5
SCALE_B = 0.125
LAM_INIT = 0.8 - 0.6 * math.exp(-0.3 * 0)
ROWS0 = HA * 64 + HB * 128
ROWS1 = HC * 64 + HD * 64


class Buf:
    __slots__ = ("writers", "dma_writers", "readers", "dma_readers")

    def __init__(self):
        self.writers = {}
        self.dma_writers = []
        self.readers = {}
        self.dma_readers = []


class Op:
    __slots__ = ("eng", "fn", "deps", "is_dma", "sem", "val", "signals", "pre", "cc")

    def __init__(self, eng, fn, is_dma, cc):
        self.eng = eng
        self.fn = fn
        self.is_dma = is_dma
        self.cc = cc
        self.deps = set()
        self.sem = None
        self.val = 0
        self.signals = False
        self.pre = None


class T:
    __slots__ = ("ap", "b")

    def __init__(self, ap, b=None):
        self.ap = ap
        self.b = b if b is not None else Buf()

    def __getitem__(self, idx):
        return T(self.ap[idx], self.b)


ENGS = ["pe", "act", "dve", "pool", "sp"]
COMPUTE = ("pe", "act", "dve", "pool")


class Prog:
    NDMA_SEMS = 8

    def __init__(self, nc):
        self.nc = nc
        self.ops = []
        self.stack = contextlib.ExitStack()
        self.last = {}
        self.pending_dma = []

    def op(self, eng, fn, reads=(), writes=(), dma=False, cc=False, after=()):
        asyn = dma or cc
        o = Op(eng, fn, asyn, cc)
        for x in after:
            if x is not None:
                o.deps.add(x)
        for t in reads:
            b = t.b
            o.deps.update(b.writers.values())
            o.deps.update(b.dma_writers)
        for t in writes:
            b = t.b
            if b.readers or b.dma_readers:
                o.deps.update(b.readers.values())
                o.deps.update(b.dma_readers)
                b.readers = {}
                b.dma_readers = []
                b.writers = {}
                b.dma_writers = []
        for t in reads:
            if asyn:
                t.b.dma_readers.append(o)
            else:
                t.b.readers[eng] = o
        for t in writes:
            if asyn:
                t.b.dma_writers.append(o)
            else:
                t.b.writers[eng] = o
        o.deps.discard(o)
        self.ops.append(o)
        if asyn:
            self.pending_dma.append(o)
        else:
            self.last[eng] = o
        return o

    def barrier(self):
        deps = list(self.last.values()) + list(self.pending_dma)
        self.pending_dma = []
        for e in ENGS:
            self.op(e, None, after=deps)

    def emit(self):
        nc = self.nc
        per = {e: [] for e in ENGS}
        for o in self.ops:
            per[o.eng].append(o)
        for o in self.ops:
            for d in o.deps:
                if d.eng == o.eng and o.eng == "pe" and not d.is_dma and not o.is_dma:
                    continue
                d.signals = True
        st = self.stack
        sems = {e: st.enter_context(nc.semaphore("s_" + e)) for e in COMPUTE}
        dma_sems = {}
        for e in ENGS:
            if any(o.is_dma and not o.cc for o in per[e]):
                dma_sems[e] = [st.enter_context(nc.semaphore("d_%s%d" % (e, i))) for i in range(self.NDMA_SEMS)]
        cc_sem = st.enter_context(nc.semaphore("s_cc")) if any(o.cc for o in self.ops) else None
        ccnt = 0
        for e in ENGS:
            cnt = 0
            dcnt = 0
            for o in per[e]:
                if o.cc:
                    o.sem = cc_sem
                    o.pre = (cc_sem, ccnt) if ccnt > 0 else None
                    ccnt += 1
                    o.val = ccnt
                    o.signals = True
                elif o.is_dma:
                    s = dma_sems[e][dcnt % self.NDMA_SEMS]
                    k = dcnt // self.NDMA_SEMS
                    o.sem = s
                    o.val = 16 * (k + 1)
                    o.pre = (s, 16 * k) if k > 0 else None
                    o.signals = True
                    dcnt += 1
                elif o.signals and o.fn is not None:
                    cnt += 1
                    o.sem = sems[e]
                    o.val = cnt
        engobj = {"pe": "tensor", "act": "scalar", "dve": "vector", "pool": "gpsimd", "sp": "sync"}

        def run(e, eo):
            waited = {}
            for o in per[e]:
                need = {}
                for d in o.deps:
                    if d.sem is None:
                        continue
                    if d.eng == e and e == "pe" and not d.is_dma and not o.is_dma:
                        continue
                    key = id(d.sem)
                    if key not in need or need[key][1] < d.val:
                        need[key] = (d.sem, d.val)
                if o.pre is not None:
                    key = id(o.pre[0])
                    if key not in need or need[key][1] < o.pre[1]:
                        need[key] = o.pre
                for key, (s, v) in need.items():
                    if waited.get(key, 0) < v:
                        eo.wait_ge(s, v)
                        waited[key] = v
                if o.fn is None:
                    continue
                ins = o.fn(eo)
                if o.cc:
                    ins.then_inc(o.sem, 1)
                elif o.is_dma:
                    ins.then_inc(o.sem, 16)
                elif o.signals:
                    ins.then_inc(o.sem, 1)

        with nc.Block() as block:
            for e in ENGS:
                if per[e]:
                    getattr(block, engobj[e])(lambda eo, e=e: run(e, eo))


class Rot:
    def __init__(self, tiles):
        self.tiles = tiles
        self.i = 0

    def next(self):
        t = self.tiles[self.i % len(self.tiles)]
        self.i += 1
        return t


class KB:
    ARENA = 51000

    def __init__(self, debug=()):
        self.nc = nc = bass.Bass("TRN2", target_bir_lowering=False)
        self.P = Prog(nc)
        self.debug = set(debug)
        st = self.P.stack
        self.arena = st.enter_context(nc.sbuf_tensor("arena", [128, self.ARENA], F32))
        self.pers = st.enter_context(nc.sbuf_tensor("pers", [128, 2048], F32))
        self.pers_off = 0
        self.off = 0
        self.psum = [T(st.enter_context(nc.psum_tensor("ps%d" % i, [128, 512], F32))[:, :]) for i in range(8)]
        self.ps_i = 0
        self.inputs = {}
        self.outs = []

    def din(self, name, shape):
        ap = self.nc.dram_tensor(name, list(shape), F32, kind="ExternalInput").ap()
        self.inputs[name] = tuple(shape)
        return ap

    def scratch(self, name, shape, dt):
        kind = "ExternalOutput" if name in self.debug else "Internal"
        return self.nc.dram_tensor(name, list(shape), dt, kind=kind).ap()

    def reset(self):
        self.P.barrier()
        self.off = 0

    def alloc(self, shape, dt=F32, persistent=False):
        n = int(np.prod(shape[1:]))
        words = n if dt == F32 else (n + 1) // 2
        if persistent:
            base = self.pers[:, self.pers_off:self.pers_off + words]
            self.pers_off += words
            assert self.pers_off <= 2048
        else:
            base = self.arena[:, self.off:self.off + words]
            self.off += words
            assert self.off <= self.ARENA, "arena overflow %d" % self.off
        if dt != F32:
            base = base.bitcast(dt)[:, 0:n]
        if len(shape) == 3:
            base = base.rearrange("p (a b) -> p a b", b=shape[2])
        elif len(shape) == 4:
            base = base.rearrange("p (a b c) -> p a b c", b=shape[2], c=shape[3])
        if shape[0] < 128:
            base = base[0:shape[0]]
        return T(base)

    def next_ps(self, lo=0, hi=8):
        i = lo + self.ps_i % (hi - lo)
        self.ps_i += 1
        return self.psum[i]

    @staticmethod
    def _rd(*xs):
        return [x for x in xs if isinstance(x, T)]

    @staticmethod
    def _a(x):
        return x.ap if isinstance(x, T) else x

    def dma(self, out, in_, q="sp", reads=(), writes=()):
        oa, ia = self._a(out), self._a(in_)
        return self.P.op(q, lambda e: e.dma_start(out=oa, in_=ia), reads=self._rd(in_) + list(reads),
                         writes=self._rd(out) + list(writes), dma=True)

    def mm(self, ps, pairs, first=True, last=True, extra_reads=()):
        n = len(pairs)
        for i, (l, r) in enumerate(pairs):
            la, ra, pa = l.ap, r.ap, ps.ap
            self.P.op("pe", lambda e, la=la, ra=ra, pa=pa, s=(first and i == 0), t=(last and i == n - 1):
                      e.matmul(pa, la, ra, start=s, stop=t), reads=[l, r], writes=[ps])

    def act(self, out, in_, func, bias=0.0, scale=1.0, eng="act"):
        oa, ia, ba, sa = out.ap, in_.ap, self._a(bias), self._a(scale)
        self.P.op("act", lambda e: e.activation(out=oa, in_=ia, func=func, bias=ba, scale=sa),
                  reads=self._rd(in_, bias, scale), writes=[out])

    def ts(self, out, in0, s1, s2, op0, op1=None, eng="dve"):
        oa, ia, a1, a2 = out.ap, in0.ap, self._a(s1), self._a(s2)
        if op1 is None:
            fn = lambda e: e.tensor_scalar(out=oa, in0=ia, scalar1=a1, scalar2=None, op0=op0)
        else:
            fn = lambda e: e.tensor_scalar(out=oa, in0=ia, scalar1=a1, scalar2=a2, op0=op0, op1=op1)
        self.P.op(eng, fn, reads=self._rd(in0, s1, s2), writes=[out])

    def stt(self, out, in0, s, in1, op0, op1, eng="dve"):
        oa, ia, sa, ib = out.ap, in0.ap, self._a(s), in1.ap
        self.P.op(eng, lambda e: e.scalar_tensor_tensor(out=oa, in0=ia, scalar=sa, in1=ib, op0=op0, op1=op1),
                  reads=self._rd(in0, s, in1), writes=[out])

    def tt(self, out, in0, in1, op, eng="dve"):
        oa, ia, ib = out.ap, in0.ap, in1.ap
        self.P.op(eng, lambda e: e.tensor_tensor(out=oa, in0=ia, in1=ib, op=op), reads=[in0, in1], writes=[out])

    def copy(self, out, in_, eng="dve"):
        oa, ia = out.ap, in_.ap
        if eng == "act":
            self.P.op("act", lambda e: e.copy(out=oa, in_=ia), reads=[in_], writes=[out])
        else:
            self.P.op(eng, lambda e: e.tensor_copy(out=oa, in_=ia), reads=[in_], writes=[out])

    def memset(self, out, v, eng="pool"):
        oa = out.ap
        self.P.op(eng, lambda e: e.memset(oa, v), writes=[out])

    def recip(self, out, in_):
        oa, ia = out.ap, in_.ap
        self.P.op("dve", lambda e: e.reciprocal(out=oa, in_=ia), reads=[in_], writes=[out])

    def load_cast(self, dst, src_ap, n, stage, q="sp", eng="pool", CH=2048):
        for c0 in range(0, n, CH):
            c1 = min(n, c0 + CH)
            sg = stage.next()
            self.dma(sg[:, 0:c1 - c0], src_ap[:, c0:c1], q=q)
            self.copy(dst[:, c0:c1], sg[:, 0:c1 - c0], eng=eng)


def flat(t):
    return T(t.ap.rearrange("p a b -> p (a b)"), t.b)


def build(debug=(), stop=None, skip=(), sel_units=None):
    kb = KB(debug)
    nc, P = kb.nc, kb.P
    xT = kb.din("xT", [128, 8, S])
    xown = kb.din("xown", [128, 8, OWN]) if PS > 1 else xT
    c8 = kb.din("c8", [128, 8])
    adaw = kb.din("adaw", [96, 128, 8 * 128])
    adab = kb.din("adab", [128, 96])
    lng = kb.din("lng", [128, 32])
    lnb = kb.din("lnb", [128, 32])
    rope = kb.din("rope", [128, 2, S])
    masks = kb.din("masks", [128, 3, 128])
    ncq = 256 + 128 + 96 + 96 + 3 * HB * 128
    w0in = kb.din("w0in", [128, 8 * ncq])
    w0uq = kb.din("w0uq", [128, 2 * 2 * HA * 96])
    w0ukv = kb.din("w0ukv", [128, 2 * HA * 64])
    vec0 = kb.din("vec0", [128, 4])
    lam4 = kb.din("lam4", [128, 4 * 64])
    t5b = kb.din("t5b", [HB, 2, 128, 128])
    t5c = kb.din("t5c", [128, HB])
    wout = kb.din("wout", [2, 128, 8 * 1024])
    ncd = 3 * HC * 64 + 3 * HD * 64 + HC
    w1in = kb.din("w1in", [128, 8 * ncd])
    bf = kb.din("bf", [HC, 1])
    relb = kb.din("relb", [HD, 2, 128, 128])
    relc = kb.din("relc", [128, HD])
    wg = kb.din("wg", [2, NFC, 128, 8 * 128])
    wv = kb.din("wv", [2, NFC, 128, 8 * 128])
    wd = kb.din("wd", [2, 128, NFC * 1024])
    convw = kb.din("convw", [128, 2 * 3 * NFC])
    convb = kb.din("convb", [128, 2 * NFC])
    outT = nc.dram_tensor("outT", [128, 8, OWN], F32, kind="ExternalOutput").ap()

    qA = kb.scratch("qA", [HA, 96, S], BF16)
    kA = kb.scratch("kA", [HA, 96, S], BF16)
    vA = kb.scratch("vA", [HA, 128, 64, 64], BF16)
    qB = kb.scratch("qB", [HB, 128, S], BF16)
    kBt = kb.scratch("kB", [HB, 128, S], BF16)
    vB = kb.scratch("vB", [HB, 128, 64, 128], BF16)
    odraw = kb.scratch("odraw", [HB, 2, 128, S], F32)
    oT0 = kb.scratch("oT0", [ROWS0, S], BF16)
    oT1 = kb.scratch("oT1", [ROWS1, S], BF16)
    x1T = kb.scratch("x1T", [128, 8, OWN], F32)
    u1T = kb.scratch("u1T", [128, 8, OWN], BF16)
    qC = kb.scratch("qC", [HC, 70, S], BF16)
    kC = kb.scratch("kC", [HC, 70, S], BF16)
    vC = kb.scratch("vC", [HC, 128, 64, 64], BF16)
    qDt = kb.scratch("qD", [HD, 64, S], BF16)
    kDt = kb.scratch("kD", [HD, 64, S], BF16)
    vDt = kb.scratch("vD", [HD, 128, 64, 64], BF16)
    wgb = kb.scratch("wgb", [2, NFC, 128, 8 * 128], BF16)
    wvb = kb.scratch("wvb", [2, NFC, 128, 8 * 128], BF16)
    zd = [T(kb.scratch("zd%d" % i, [1, GW], F32)) for i in range(4)]
    zrot = Rot(zd)

    mods = kb.alloc([128, 96], persistent=True)
    sc1 = kb.alloc([128, 96], persistent=True)
    lng_t = kb.alloc([128, 32], persistent=True)
    lnb_t = kb.alloc([128, 32], persistent=True)
    ones_f = kb.alloc([128, 128], persistent=True)
    vec0_t = kb.alloc([128, 4], persistent=True)
    t5c_t = kb.alloc([128, HB], persistent=True)
    relc_t = kb.alloc([128, HD], persistent=True)
    lamc = kb.alloc([128, 4], persistent=True)
    convw_t = kb.alloc([128, 2 * 3 * NFC], persistent=True)
    convb_t = kb.alloc([128, 2 * NFC], persistent=True)
    bf_t = kb.alloc([128, 1], persistent=True)
    carry = kb.alloc([128, 1], persistent=True)
    mask_b = kb.alloc([128, 3, 128], BF16, persistent=True)
    ones_b = kb.alloc([128, GW], BF16, persistent=True)
    ghalo = kb.alloc([128, NFC, 2], persistent=True)

    def mcol(m, which, kc):
        c = m * 24 + which * 8 + kc
        src = mods if which == 0 else sc1
        return src[:, c:c + 1]

    for dst, src in ((lng_t, lng), (lnb_t, lnb), (vec0_t, vec0), (t5c_t, t5c), (relc_t, relc),
                     (convw_t, convw), (convb_t, convb)):
        kb.dma(dst, src)
    kb.dma(bf_t[0:HC, :], bf)
    kb.memset(ones_f, 1.0)
    kb.memset(ones_b, 1.0)
    kb.memset(ghalo, 0.0)
    kb.memset(carry, 0.0)
    cond = kb.alloc([128, 8])
    adab_t = kb.alloc([128, 96])
    mstage = kb.alloc([128, 3, 128])
    lam_t = kb.alloc([128, 4, 64])
    lam_p = kb.alloc([128, 2, 64])
    lam_s = kb.alloc([128, 2])
    kb.dma(cond, c8)
    kb.dma(adab_t, adab)
    kb.dma(mstage, masks)
    kb.dma(flat(lam_t), lam4)
    kb.copy(mask_b, mstage, eng="pool")
    kb.act(cond, cond, AF.Silu)
    kb.tt(lam_p[:, 0, :], lam_t[:, 0, :], lam_t[:, 1, :], ALU.mult)
    kb.tt(lam_p[:, 1, :], lam_t[:, 2, :], lam_t[:, 3, :], ALU.mult)
    for i in range(2):
        oa, ia = lam_s[:, i:i + 1].ap, lam_p[:, i, :].ap
        P.op("dve", lambda e, oa=oa, ia=ia: e.reduce_sum(out=oa, in_=ia, axis=mybir.AxisListType.X),
             reads=[lam_p], writes=[lam_s])
    kb.act(lam_s, lam_s, AF.Exp)
    kb.tt(lamc[:, 0:1], lam_s[:, 1:2], lam_s[:, 0:1], ALU.subtract)
    kb.ts(lamc[:, 0:1], lamc[:, 0:1], -LAM_INIT, None, ALU.add)
    kb.ts(lamc[:, 1:2], vec0_t[:, 3:4], 1.0 - LAM_INIT, None, ALU.mult)
    wrot = Rot([kb.alloc([128, 8, 128]) for _ in range(3)])
    psm = kb.next_ps()
    for col in range(96):
        wt = wrot.next()
        kb.dma(flat(wt), adaw[col])
        kb.mm(psm[:, col:col + 1], [(wt[:, ic, :], cond[:, ic:ic + 1]) for ic in range(8)])
    kb.tt(mods, psm[:, 0:96], adab_t, ALU.add)
    kb.ts(sc1, mods, 1.0, None, ALU.add)

    def cast_ffn_weights(layer):
        stg = Rot([kb.alloc([128, 1024]) for _ in range(2)])
        stb = Rot([kb.alloc([128, 1024], BF16) for _ in range(2)])
        for fc in range(NFC):
            for src, dst in ((wg, wgb), (wv, wvb)):
                a = stg.next()
                b_ = stb.next()
                kb.dma(a, src[layer, fc], q="pool")
                kb.copy(b_, a, eng="pool")
                kb.dma(dst[layer, fc], b_, q="pool")

    def attention_phase(units):
        kb.reset()
        kt = [kb.alloc([128, S], BF16) for _ in range(2)]
        vt = [kb.alloc([128, 64, 129], BF16) for _ in range(2)]
        qg = Rot([kb.alloc([128, GW], BF16) for _ in range(3)])
        pt = Rot([kb.alloc([128, GW], BF16) for _ in range(3)])
        tmp = Rot([kb.alloc([128, GW]) for _ in range(2)])
        rz = Rot([kb.alloc([128, GW]) for _ in range(2)])
        bc = Rot([kb.alloc([64, GW]) for _ in range(2)])
        ot = Rot([kb.alloc([64, GW]) for _ in range(4)])
        btiles = [[kb.alloc([128, 128]) for _ in range(2)] for _ in range(2)]
        for v in vt:
            kb.memset(v[:, :, 64:65], 1.0)

        def load_unit(ui):
            u = units[ui]
            R = u["R"]
            k_, v_ = kt[ui % 2], vt[ui % 2]
            for c in range(4):
                kb.dma(k_[0:R, c * 2048:(c + 1) * 2048], u["kT"][:, c * 2048:(c + 1) * 2048])
            kb.dma(v_[:, :, 0:64], u["vsrc"][0])
            if len(u["vsrc"]) > 1:
                kb.dma(v_[:, :, 65:129], u["vsrc"][1])
            if u["bias"] is not None:
                kb.dma(btiles[ui % 2][0], u["bias"][0])
                kb.dma(btiles[ui % 2][1], u["bias"][1])

        def load_q(ui, G):
            u = units[ui]
            t = qg.next()
            kb.dma(t[0:u["R"], :], u["qT"][:, G * GW:(G + 1) * GW])
            return t

        seq = [(ui, G) for ui in range(len(units)) for G in range(NG)]
        load_unit(0)
        qnext = load_q(0, 0)
        for si, (ui, G) in enumerate(seq):
            u = units[ui]
            R, nv, scale, kind = u["R"], len(u["vsrc"]), u["scale"], u["kind"]
            k_, v_ = kt[ui % 2], vt[ui % 2]
            qcur = qnext
            if G == 0 and ui + 1 < len(units):
                load_unit(ui + 1)
            if si + 1 < len(seq):
                qnext = load_q(*seq[si + 1])
            plan = []
            if kind in ("cc", "fc"):
                for kbi in range(4 * G):
                    plan.append((kbi, 0, GW, None))
                for j in range(4):
                    plan.append((4 * G + j, 128 * j, GW, (128 * j, 0 if kind == "cc" else 1)))
            else:
                for r in (-1, -2, -3, -4):
                    if 4 * G + r >= 0:
                        c1 = 64 * (2 * r + 10)
                        plan.append((4 * G + r, 0, c1, (c1 - 128, 2)))
                for r in range(4):
                    plan.append((4 * G + r, 128 * r, GW, (128 * r, 0)))
            accs = [kb.psum[4 + 2 * (si % 2) + v] for v in range(nv)]
            for pi, (kbi, c0, c1, mk) in enumerate(plan):
                s = kb.next_ps(0, 4)
                kb.mm(s[:, c0:c1], [(k_[0:R, kbi * 128:(kbi + 1) * 128], qcur[0:R, c0:c1])])
                p = pt.next()
                runs = []
                if u["bias"] is not None:
                    bt = btiles[ui % 2]
                    cb = u["bias"][2]
                    cur = None
                    for cc0 in range(c0, c1, 128):
                        dlt = (4 * G + cc0 // 128) - kbi
                        if dlt in (0, 1):
                            tm = tmp.next()
                            kb.stt(tm[:, cc0:cc0 + 128], s[:, cc0:cc0 + 128], scale, bt[dlt], ALU.mult, ALU.add)
                            kb.act(p[:, cc0:cc0 + 128], tm[:, cc0:cc0 + 128], AF.Exp)
                            cur = None
                        else:
                            if cur is None:
                                cur = [cc0, cc0 + 128]
                                runs.append(cur)
                            else:
                                cur[1] = cc0 + 128
                    for a0, a1 in runs:
                        kb.act(p[:, a0:a1], s[:, a0:a1], AF.Exp, bias=cb, scale=scale)
                else:
                    kb.act(p[:, c0:c1], s[:, c0:c1], AF.Exp, scale=scale)
                if mk is not None:
                    m0, mi = mk
                    kb.tt(p[:, m0:m0 + 128], p[:, m0:m0 + 128], mask_b[:, mi, :], ALU.mult, eng="pool")
                for v in range(nv):
                    vc0, vc1 = (0, 65) if v == 0 else (65, 129)
                    kb.mm(accs[v][0:vc1 - vc0, c0:c1], [(v_[:, kbi, vc0:vc1], p[:, c0:c1])],
                          first=(pi == 0), last=(pi == len(plan) - 1))
            rzt = rz.next()
            kb.recip(rzt[64:65, :], accs[0][64:65, :])
            zslot = zrot.next()
            kb.dma(zslot, rzt[64:65, :])
            bct = bc.next()
            kb.dma(bct, T(zslot.ap.broadcast_to([64, GW]), zslot.b))
            for v in range(nv):
                o_ = ot.next()
                if u["odt"] == BF16:
                    ob = T(o_.ap.bitcast(BF16)[:, 0:GW], o_.b)
                else:
                    ob = o_
                kb.tt(ob, accs[v][0:64, :], bct, ALU.mult)
                kb.dma(u["outs"][v][:, G * GW:(G + 1) * GW], ob)

    def phase_a0():
        kb.reset()
        cast_stage = Rot([kb.alloc([128, 2048]) for _ in range(2)])
        W = kb.alloc([128, 8, ncq], BF16)
        Wuq = kb.alloc([128, 2, 2 * HA * 96], BF16)
        Wukv = kb.alloc([128, 2 * HA * 64], BF16)
        kb.load_cast(flat(W), w0in, 8 * ncq, cast_stage)
        kb.load_cast(flat(Wuq), w0uq, 2 * 2 * HA * 96, cast_stage)
        kb.load_cast(Wukv, w0ukv, 2 * HA * 64, cast_stage)
        o_cq, o_ckv, o_kr, o_krs = 0, 256, 384, 480
        o_dq, o_dk, o_dv = 576, 576 + HB * 128, 576 + 2 * HB * 128
        xg = Rot([kb.alloc([128, 8, GW]) for _ in range(2)])
        rp = Rot([kb.alloc([128, 2, GW]) for _ in range(2)])
        ug = Rot([kb.alloc([128, 8, GW], BF16) for _ in range(2)])
        cq_sb = kb.alloc([128, 2, GW])
        sq = Rot([kb.alloc([128, GW]) for _ in range(2)])
        rs = Rot([kb.alloc([128, GW]) for _ in range(2)])
        cqn = kb.alloc([128, 2, GW], BF16)
        ckv_sb = kb.alloc([128, GW])
        ckvn = kb.alloc([128, GW], BF16)
        qh = Rot([kb.alloc([128, GW], BF16) for _ in range(3)])
        t1 = Rot([kb.alloc([128, GW]) for _ in range(2)])
        t2 = Rot([kb.alloc([128, GW]) for _ in range(2)])
        krope = kb.alloc([128, GW], BF16)
        kn = Rot([kb.alloc([64, GW], BF16) for _ in range(3)])
        v_sb = kb.alloc([128, 4, HA * 64], BF16)
        dqk = Rot([kb.alloc([128, GW], BF16) for _ in range(3)])
        vd_sb = kb.alloc([128, 4, HB * 128], BF16)

        def loads(g):
            x_ = xg.next()
            r_ = rp.next()
            kb.dma(x_, xT[:, :, g * GW:(g + 1) * GW])
            kb.dma(r_[64:96], rope[64:96, :, g * GW:(g + 1) * GW])
            return x_, r_

        nxt = loads(0)
        for g in range(NG):
            x_, r_ = nxt
            if g + 1 < NG:
                nxt = loads(g + 1)
            cs = slice(g * GW, (g + 1) * GW)
            u_ = ug.next()
            for kc in range(8):
                kb.ts(u_[:, kc, :], x_[:, kc, :], mcol(0, 1, kc), mcol(0, 0, kc), ALU.mult, ALU.add)
            pss = kb.next_ps()
            for mc in range(2):
                ps = kb.next_ps()
                kb.mm(ps, [(W[:, kc, o_cq + mc * 128:o_cq + (mc + 1) * 128], u_[:, kc, :]) for kc in range(8)])
                kb.copy(cq_sb[:, mc, :], ps, eng="act")
                s_ = sq.next()
                kb.act(s_, ps, AF.Square)
                kb.mm(pss, [(ones_f, s_)], first=(mc == 0), last=(mc == 1))
            r1 = rs.next()
            kb.act(r1, pss, AF.Sqrt, bias=EPS_RMS, scale=1.0 / 256)
            kb.recip(r1, r1)
            for mc in range(2):
                kb.stt(cqn[:, mc, :], cq_sb[:, mc, :], vec0_t[:, mc:mc + 1], r1, ALU.mult, ALU.mult)
            ps = kb.next_ps()
            kb.mm(ps, [(W[:, kc, o_ckv:o_ckv + 128], u_[:, kc, :]) for kc in range(8)])
            kb.copy(ckv_sb, ps, eng="act")
            s_ = sq.next()
            kb.act(s_, ps, AF.Square)
            pss = kb.next_ps()
            kb.mm(pss, [(ones_f, s_)])
            r2 = rs.next()
            kb.act(r2, pss, AF.Sqrt, bias=EPS_RMS, scale=1.0 / 128)
            kb.recip(r2, r2)
            kb.stt(ckvn, ckv_sb, vec0_t[:, 2:3], r2, ALU.mult, ALU.mult)
            for h in range(HA):
                ps1 = kb.next_ps()
                ps2 = kb.next_ps()
                kb.mm(ps1[0:96, :], [(Wuq[:, k2, h * 96:(h + 1) * 96], cqn[:, k2, :]) for k2 in range(2)])
                kb.mm(ps2[0:96, :], [(Wuq[:, k2, HA * 96 + h * 96:HA * 96 + (h + 1) * 96], cqn[:, k2, :]) for k2 in range(2)])
                q_ = qh.next()
                a_, b_ = t1.next(), t2.next()
                kb.copy(q_[0:64, :], ps1[0:64, :], eng="act")
                kb.tt(a_[64:96, :], ps1[64:96, :], r_[64:96, 0, :], ALU.mult)
                kb.tt(b_[64:96, :], ps2[64:96, :], r_[64:96, 1, :], ALU.mult)
                kb.tt(q_[64:96, :], a_[64:96, :], b_[64:96, :], ALU.add)
                kb.dma(qA[h][:, cs], q_[0:96, :])
            ps1 = kb.next_ps()
            ps2 = kb.next_ps()
            kb.mm(ps1[0:96, :], [(W[:, kc, o_kr:o_kr + 96], u_[:, kc, :]) for kc in range(8)])
            kb.mm(ps2[0:96, :], [(W[:, kc, o_krs:o_krs + 96], u_[:, kc, :]) for kc in range(8)])
            a_, b_ = t1.next(), t2.next()
            kb.tt(a_[64:96, :], ps1[64:96, :], r_[64:96, 0, :], ALU.mult)
            kb.tt(b_[64:96, :], ps2[64:96, :], r_[64:96, 1, :], ALU.mult)
            kb.tt(krope[64:96, :], a_[64:96, :], b_[64:96, :], ALU.add)
            for h in range(HA):
                kb.dma(kA[h][64:96, cs], krope[64:96, :])
            for h in range(HA):
                ps = kb.next_ps()
                kb.mm(ps[0:64, :], [(Wukv[:, h * 64:(h + 1) * 64], ckvn)])
                k_ = kn.next()
                kb.copy(k_, ps[0:64, :], eng="act")
                kb.dma(kA[h][0:64, cs], k_)
            for blk in range(4):
                ps = kb.next_ps()
                kb.mm(ps[:, 0:HA * 64], [(ckvn[:, blk * 128:(blk + 1) * 128], Wukv[:, HA * 64:2 * HA * 64])])
                kb.copy(v_sb[:, blk, :], ps[:, 0:HA * 64], eng="dve")
            for h in range(HA):
                kb.dma(vA[h][:, 4 * g:4 * g + 4, :], v_sb[:, :, h * 64:(h + 1) * 64])
            for h in range(HB):
                for off, dst in ((o_dq, qB), (o_dk, kBt)):
                    ps = kb.next_ps()
                    kb.mm(ps, [(W[:, kc, off + h * 128:off + (h + 1) * 128], u_[:, kc, :]) for kc in range(8)])
                    t_ = dqk.next()
                    kb.copy(t_, ps, eng="act")
                    kb.dma(dst[h][:, cs], t_)
            for blk in range(4):
                ps = kb.next_ps()
                kb.mm(ps[:, 0:HB * 128], [(u_[:, kc, blk * 128:(blk + 1) * 128], W[:, kc, o_dv:o_dv + HB * 128])
                                          for kc in range(8)])
                kb.copy(vd_sb[:, blk, :], ps[:, 0:HB * 128], eng="dve")
            for h in range(HB):
                kb.dma(vB[h][:, 4 * g:4 * g + 4, :], vd_sb[:, :, h * 128:(h + 1) * 128])

    def phase_c0():
        kb.reset()
        o0 = Rot([kb.alloc([128, GW]) for _ in range(2)])
        o1 = Rot([kb.alloc([128, GW]) for _ in range(2)])
        dd = Rot([kb.alloc([128, GW]) for _ in range(2)])
        sq = Rot([kb.alloc([128, GW]) for _ in range(2)])
        rs = Rot([kb.alloc([128, GW]) for _ in range(2)])
        ob = Rot([kb.alloc([128, GW], BF16) for _ in range(2)])
        for h in range(HB):
            for g in range(NG):
                cs = slice(g * GW, (g + 1) * GW)
                a, b_ = o0.next(), o1.next()
                kb.dma(a, odraw[h, 0][:, cs])
                kb.dma(b_, odraw[h, 1][:, cs])
                d_ = dd.next()
                kb.stt(d_, b_, lamc[:, 0:1], a, ALU.mult, ALU.add)
                s_ = sq.next()
                kb.act(s_, d_, AF.Square)
                ps = kb.next_ps()
                kb.mm(ps, [(ones_f, s_)])
                r_ = rs.next()
                kb.act(r_, ps, AF.Sqrt, bias=EPS_RMS, scale=1.0 / 128)
                kb.recip(r_, r_)
                o_ = ob.next()
                kb.stt(o_, d_, lamc[:, 1:2], r_, ALU.mult, ALU.mult)
                kb.dma(oT0[HA * 64 + h * 128:HA * 64 + (h + 1) * 128, cs], o_)

    def layer_norm(z, m_idx, xout, extra):
        sqr = Rot(extra["sq"])
        psm, psq = kb.next_ps(), kb.next_ps()
        for kc in range(8):
            s_ = sqr.next()
            kb.act(s_, z[:, kc, :], AF.Square)
            kb.mm(psm, [(ones_f, z[:, kc, :])], first=(kc == 0), last=(kc == 7))
            kb.mm(psq, [(ones_f, s_)], first=(kc == 0), last=(kc == 7))
        mean, msq, rstd = extra["st"]
        kb.ts(mean, psm, 1.0 / D, None, ALU.mult)
        kb.tt(msq, mean, mean, ALU.mult)
        kb.stt(rstd, psq, 1.0 / D, msq, ALU.mult, ALU.subtract)
        kb.act(rstd, rstd, AF.Sqrt, bias=EPS_LN)
        kb.recip(rstd, rstd)
        for kc in range(8):
            kb.tt(z[:, kc, :], z[:, kc, :], mean, ALU.subtract)
            kb.tt(z[:, kc, :], z[:, kc, :], rstd, ALU.mult)
            kb.ts(xout[:, kc, :], z[:, kc, :], lng_t[:, m_idx * 8 + kc:m_idx * 8 + kc + 1],
                  lnb_t[:, m_idx * 8 + kc:m_idx * 8 + kc + 1], ALU.mult, ALU.add)

    def phase_b(layer, o_all, rows):
        kb.reset()
        m_mix, m_ffn = 2 * layer, 2 * layer + 1
        cast_stage = Rot([kb.alloc([128, 1024]) for _ in range(2)])
        Wo = kb.alloc([128, 8, 1024], BF16)
        Wd = kb.alloc([128, NFC, 1024], BF16)
        kb.load_cast(flat(Wo), wout[layer], 8 * 1024, cast_stage, CH=1024)
        kb.load_cast(flat(Wd), wd[layer], NFC * 1024, cast_stage, CH=1024)
        wgv = Rot([kb.alloc([128, 2, 8, 128], BF16) for _ in range(3)])
        og = Rot([kb.alloc([128, 8, GW], BF16) for _ in range(2)])
        xr = Rot([kb.alloc([128, 8, GW]) for _ in range(1)])
        z = kb.alloc([128, 8, GW])
        xn = kb.alloc([128, 8, GW])
        u2 = kb.alloc([128, 8, GW], BF16)
        hT = kb.alloc([128, NFC, GW], BF16)
        lnx = {"sq": [kb.alloc([128, GW]) for _ in range(2)], "st": [kb.alloc([128, GW]) for _ in range(3)]}
        gsb = Rot([kb.alloc([128, GW + 2]) for _ in range(2)])
        acc = Rot([kb.alloc([128, GW]) for _ in range(2)])
        sg = Rot([kb.alloc([128, GW]) for _ in range(2)])
        xsrc = xown if layer == 0 else x1T
        kb.memset(ghalo, 0.0)

        def loads(g):
            o_, x_ = og.next(), xr.next()
            cs = slice(g * GW, (g + 1) * GW)
            for kc in range(8):
                kb.dma(o_[:, kc, :], o_all[kc * 128:(kc + 1) * 128, cs])
            kb.dma(x_, xsrc[:, :, cs])
            return o_, x_

        nxt = loads(0)
        for g in range(NGO):
            o_, x_ = nxt
            cs = slice(g * GW, (g + 1) * GW)
            kb.act(flat(x_), flat(x_), AF.Identity, scale=ALPHA)
            for mc in range(8):
                ps = kb.next_ps()
                kb.mm(ps, [(Wo[:, kc, mc * 128:(mc + 1) * 128], o_[:, kc, :]) for kc in range(8)])
                kb.stt(z[:, mc, :], ps, mcol(m_mix, 2, mc), x_[:, mc, :], ALU.mult, ALU.add)
            if g + 1 < NGO:
                nxt = loads(g + 1)
            layer_norm(z, m_mix, xn, lnx)
            for kc in range(8):
                kb.act(u2[:, kc, :], xn[:, kc, :], AF.Identity, bias=mcol(m_ffn, 0, kc), scale=mcol(m_ffn, 1, kc))
            kb.act(flat(xn), flat(xn), AF.Identity, scale=ALPHA)
            for fc in range(NFC):
                w_ = wgv.next()
                kb.dma(flat(w_[:, 0]), wgb[layer, fc])
                kb.dma(flat(w_[:, 1]), wvb[layer, fc])
                psg, psv = kb.next_ps(), kb.next_ps()
                kb.mm(psg, [(w_[:, 0, kc, :], u2[:, kc, :]) for kc in range(8)])
                kb.mm(psv, [(w_[:, 1, kc, :], u2[:, kc, :]) for kc in range(8)])
                g_ = gsb.next()
                kb.copy(g_[:, 0:2], ghalo[:, fc, :], eng="pool")
                kb.copy(g_[:, 2:GW + 2], psg, eng="act")
                kb.copy(ghalo[:, fc, :], g_[:, GW:GW + 2], eng="pool")
                a_ = acc.next()
                cw = lambda j: convw_t[:, (layer * 3 + j) * NFC + fc:(layer * 3 + j) * NFC + fc + 1]
                kb.ts(a_, g_[:, 0:GW], cw(0), convb_t[:, layer * NFC + fc:layer * NFC + fc + 1], ALU.mult, ALU.add)
                kb.stt(a_, g_[:, 1:GW + 1], cw(1), a_, ALU.mult, ALU.add)
                kb.stt(a_, g_[:, 2:GW + 2], cw(2), a_, ALU.mult, ALU.add)
                s_ = sg.next()
                kb.act(s_, a_, AF.Silu)
                kb.tt(hT[:, fc, :], s_, psv, ALU.mult)
            for mc in range(8):
                ps = kb.next_ps()
                kb.mm(ps, [(Wd[:, fc, mc * 128:(mc + 1) * 128], hT[:, fc, :]) for fc in range(NFC)])
                kb.stt(z[:, mc, :], ps, mcol(m_ffn, 2, mc), xn[:, mc, :], ALU.mult, ALU.add)
            layer_norm(z, m_ffn, xn, lnx)
            if layer == 0:
                kb.dma(x1T[:, :, cs], xn)
                u_ = u2
                for kc in range(8):
                    kb.act(u_[:, kc, :], xn[:, kc, :], AF.Identity, bias=mcol(2, 0, kc), scale=mcol(2, 1, kc))
                kb.dma(u1T[:, :, cs], u_)
            else:
                kb.dma(outT[:, :, cs], xn)

    def phase_a1(u_all):
        kb.reset()
        cast_stage = Rot([kb.alloc([128, 2048]) for _ in range(2)])
        W = kb.alloc([128, 8, ncd], BF16)
        kb.load_cast(flat(W), w1in, 8 * ncd, cast_stage)
        o_cq, o_ck, o_cv = 0, HC * 64, 2 * HC * 64
        o_dq, o_dk, o_dv = 3 * HC * 64, 3 * HC * 64 + HD * 64, 3 * HC * 64 + 2 * HD * 64
        o_fl = 3 * HC * 64 + 3 * HD * 64
        ug = Rot([kb.alloc([128, 8, GW], BF16) for _ in range(2)])
        qk = Rot([kb.alloc([128, GW], BF16) for _ in range(3)])
        vc_sb = kb.alloc([128, 4, HC * 64], BF16)
        vd_sb = kb.alloc([128, 4, HD * 64], BF16)
        e_ = kb.alloc([128, GW])
        csA = kb.alloc([128, GW])
        csB = kb.alloc([128, GW])
        f8 = kb.alloc([128, GW])
        r1 = kb.alloc([128, GW])
        fp_ = Rot([kb.alloc([128, 6, GW], BF16) for _ in range(2)])
        kb.memset(carry, 0.0)

        def loads(g):
            u_ = ug.next()
            kb.dma(u_, u_all[:, :, g * GW:(g + 1) * GW])
            return u_

        nxt = loads(0)
        for g in range(NG):
            u_ = nxt
            if g + 1 < NG:
                nxt = loads(g + 1)
            cs = slice(g * GW, (g + 1) * GW)
            for off, dst, nh in ((o_cq, qC, HC), (o_ck, kC, HC), (o_dq, qDt, HD), (o_dk, kDt, HD)):
                for hp in range(nh // 2):
                    ps = kb.next_ps()
                    kb.mm(ps, [(W[:, kc, off + hp * 128:off + (hp + 1) * 128], u_[:, kc, :]) for kc in range(8)])
                    t_ = qk.next()
                    kb.copy(t_, ps, eng="act")
                    kb.dma(dst[2 * hp][0:64, cs], t_[0:64, :])
                    kb.dma(dst[2 * hp + 1][0:64, cs], t_[64:128, :])
            for off, sb, dst, nh in ((o_cv, vc_sb, vC, HC), (o_dv, vd_sb, vDt, HD)):
                for blk in range(4):
                    ps = kb.next_ps()
                    kb.mm(ps[:, 0:nh * 64], [(u_[:, kc, blk * 128:(blk + 1) * 128], W[:, kc, off:off + nh * 64])
                                             for kc in range(8)])
                    kb.copy(sb[:, blk, :], ps[:, 0:nh * 64], eng="dve")
                for h in range(nh):
                    kb.dma(dst[h][:, 4 * g:4 * g + 4, :], sb[:, :, h * 64:(h + 1) * 64])
            ps = kb.next_ps()
            kb.mm(ps[0:HC, :], [(W[:, kc, o_fl:o_fl + HC], u_[:, kc, :]) for kc in range(8)])
            kb.ts(e_[0:HC, :], ps[0:HC, :], bf_t[0:HC, :], -1.0, ALU.add, ALU.mult)
            kb.act(e_[0:HC, :], e_[0:HC, :], AF.Exp)
            kb.act(csA[0:HC, :], e_[0:HC, :], AF.Ln, bias=1.0)
            a, b_ = csA, csB
            sft = 1
            while sft < GW:
                kb.copy(b_[0:HC, 0:sft], a[0:HC, 0:sft], eng="dve")
                kb.tt(b_[0:HC, sft:GW], a[0:HC, sft:GW], a[0:HC, 0:GW - sft], ALU.add)
                a, b_ = b_, a
                sft *= 2
            kb.ts(f8[0:HC, :], a[0:HC, :], carry[0:HC, :], -8.0, ALU.add, ALU.mult)
            kb.tt(carry[0:HC, :], carry[0:HC, :], a[0:HC, GW - 1:GW], ALU.add)
            fp = fp_.next()
            kb.copy(fp[0:HC, 0, :], f8[0:HC, :], eng="dve")
            kb.tt(r1[0:HC, :], f8[0:HC, :], fp[0:HC, 0, :], ALU.subtract)
            kb.copy(fp[0:HC, 1, :], r1[0:HC, :], eng="dve")
            kb.tt(r1[0:HC, :], r1[0:HC, :], fp[0:HC, 1, :], ALU.subtract)
            kb.copy(fp[0:HC, 2, :], r1[0:HC, :], eng="dve")
            for j in range(3):
                kb.ts(fp[0:HC, 3 + j, :], fp[0:HC, j, :], -1.0, None, ALU.mult)
            for h in range(HC):
                kb.dma(qC[h:h + 1, 64:67, cs], fp[h:h + 1, 0:3, :])
                kb.dma(kC[h:h + 1, 67:70, cs], fp[h:h + 1, 3:6, :])
                kb.dma(qC[h][67:70, cs], ones_b[0:3, :])
                kb.dma(kC[h][64:67, cs], ones_b[0:3, :])

    t5_tiles = [(T(t5b[h, 0]), T(t5b[h, 1])) for h in range(HB)]
    rel_tiles = [(T(relb[h, 0]), T(relb[h, 1])) for h in range(HD)]
    units0 = []
    for h in range(HA):
        units0.append(dict(qT=qA[h], kT=kA[h], R=96, vsrc=[vA[h]], scale=SCALE_A, kind="cc", bias=None,
                           outs=[oT0[h * 64:(h + 1) * 64, :]], odt=BF16))
    for h in range(HB):
        for c in range(2):
            units0.append(dict(qT=qB[h][64 * c:64 * c + 64, :], kT=kBt[h][64 * c:64 * c + 64, :], R=64,
                               vsrc=[vB[h][:, :, 0:64], vB[h][:, :, 64:128]], scale=SCALE_B, kind="cc",
                               bias=(t5_tiles[h][0], t5_tiles[h][1], t5c_t[:, h:h + 1]),
                               outs=[odraw[h, c][0:64, :], odraw[h, c][64:128, :]], odt=F32))
    units1 = []
    for h in range(HC):
        units1.append(dict(qT=qC[h], kT=kC[h], R=70, vsrc=[vC[h]], scale=SCALE_B, kind="fc", bias=None,
                           outs=[oT1[h * 64:(h + 1) * 64, :]], odt=BF16))
    for h in range(HD):
        units1.append(dict(qT=qDt[h], kT=kDt[h], R=64, vsrc=[vDt[h]], scale=SCALE_B, kind="band",
                           bias=(rel_tiles[h][0], rel_tiles[h][1], relc_t[:, h:h + 1]),
                           outs=[oT1[HC * 64 + h * 64:HC * 64 + (h + 1) * 64, :]], odt=BF16))
    if sel_units is not None:
        units0 = [units0[i] for i in sel_units[0]]
        units1 = [units1[i] for i in sel_units[1]]
    steps = [("a0", phase_a0), ("cast", lambda: (cast_ffn_weights(0), cast_ffn_weights(1))),
             ("att0", lambda: attention_phase(units0)), ("c0", phase_c0), ("b0", lambda: phase_b(0, oT0, ROWS0)),
             ("a1", lambda: phase_a1(u1T)), ("att1", lambda: attention_phase(units1)),
             ("b1", lambda: phase_b(1, oT1, ROWS1))]
    for name, fn in steps:
        if skip and name in skip:
            continue
        fn()
        if stop == name:
            break
    kb.reset()
    P.emit()
    P.stack.close()
    return nc, kb


def _t5_bucket(rel):
    half, max_exact = 16, 8
    n = np.abs(rel)
    large = max_exact + (np.log(np.maximum(n, 1).astype(np.float32) / max_exact)
                         / math.log(128 / max_exact) * (half - max_exact)).astype(np.int32)
    large = np.minimum(large, half - 1)
    return np.where(rel > 0, half, 0) + np.where(n < max_exact, n, large)


def _pk(w):
    K, N = w.shape
    return np.ascontiguousarray(w.reshape(K // 128, 128, N).transpose(1, 0, 2).reshape(128, (K // 128) * N))


def _col8(v):
    return np.ascontiguousarray(v.reshape(8, 128).T)


def prep_inputs(inp, core):
    f = np.float32
    b, r = core // PS, core % PS
    x = inp["x"][b]
    m = {}
    xTa = np.ascontiguousarray(x.T.reshape(8, 128, S).transpose(1, 0, 2))
    m["xT"] = xTa
    if PS > 1:
        m["xown"] = np.ascontiguousarray(xTa[:, :, r * OWN:(r + 1) * OWN])
    m["c8"] = _col8(inp["c"][b])
    aw = inp["ada_w"].reshape(4, 8, 128, 24, 128)
    m["adaw"] = np.ascontiguousarray(aw.transpose(0, 3, 2, 1, 4).reshape(96, 128, 8 * 128))
    ab = inp["ada_b"].reshape(4, 24, 128)
    m["adab"] = np.ascontiguousarray(ab.transpose(2, 0, 1).reshape(128, 96))
    m["lng"] = np.ascontiguousarray(inp["ln_g"].reshape(4, 8, 128).transpose(2, 0, 1).reshape(128, 32))
    m["lnb"] = np.ascontiguousarray(inp["ln_b"].reshape(4, 8, 128).transpose(2, 0, 1).reshape(128, 32))
    half = 16
    inv = np.power(np.float32(10000.0), -np.arange(half, dtype=f) / half).astype(f)
    ang = np.arange(S, dtype=f)[:, None] * inv[None, :]
    cos, sin = np.cos(ang).astype(f).T, np.sin(ang).astype(f).T
    rp = np.zeros((128, 2, S), f)
    rp[64:96, 0] = np.concatenate([cos, cos], 0)
    rp[64:96, 1] = np.concatenate([-sin, sin], 0)
    m["rope"] = rp
    k = np.arange(128)[:, None]
    q = np.arange(128)[None, :]
    mk = np.ones((128, 3, 128), f)
    mk[:, 0] = np.where((k >= 64) & (q < 64), 0.0, 1.0)
    mk[:, 1] = np.where(k <= q, 1.0, 0.0)
    mk[:, 2] = np.where((k < 64) & (q >= 64), 0.0, 1.0)
    m["masks"] = mk
    w = inp["ab_w_in"][0]
    ha = list(range(r * HA, (r + 1) * HA))
    hb = list(range(r * HB, (r + 1) * HB))
    z64 = np.zeros((D, 64), f)
    kr = w[:, 384:416]
    cols = [w[:, 0:256], w[:, 256:384], z64, kr, z64, kr[:, 16:32], kr[:, 0:16]]
    for base in (416, 928, 1440):
        cols += [w[:, base + h * 128:base + (h + 1) * 128] for h in hb]
    m["w0in"] = _pk(np.concatenate(cols, 1))
    uq = inp["mla_w_uq"][0]
    a1 = [uq[:, h * 96:(h + 1) * 96] for h in ha]
    a2 = [np.concatenate([uq[:, h * 96:h * 96 + 64], uq[:, h * 96 + 80:h * 96 + 96], uq[:, h * 96 + 64:h * 96 + 80]], 1)
          for h in ha]
    m["w0uq"] = _pk(np.concatenate(a1 + a2, 1))
    ukv = inp["mla_w_ukv"][0]
    m["w0ukv"] = _pk(np.concatenate([ukv[:, h * 128:h * 128 + 64] for h in ha] +
                                    [ukv[:, h * 128 + 64:h * 128 + 128] for h in ha], 1))
    v0 = np.zeros((128, 4), f)
    v0[:, 0] = inp["mla_q_norm"][0, 0:128]
    v0[:, 1] = inp["mla_q_norm"][0, 128:256]
    v0[:, 2] = inp["mla_kv_norm"][0]
    v0[:, 3] = inp["diff_sub_g"][0]
    m["vec0"] = v0
    lam = np.concatenate([inp["diff_lq1"][0], inp["diff_lk1"][0], inp["diff_lq2"][0], inp["diff_lk2"][0]])
    m["lam4"] = np.ascontiguousarray(np.broadcast_to(lam[None, :], (128, 256)))
    t5 = inp["t5_table"]
    bd = _t5_bucket(k - q)
    bp = _t5_bucket((k - 128) - q)
    m["t5b"] = np.ascontiguousarray(np.stack([np.stack([t5[h][bd], t5[h][bp]]) for h in hb]))
    m["t5c"] = np.ascontiguousarray(np.broadcast_to(t5[hb, 15][None, :], (128, HB)))
    wo = inp["ab_w_out"][0]
    rows0 = np.concatenate([wo[h * 64:(h + 1) * 64] for h in range(8)] + [wo[512 + h * 128:512 + (h + 1) * 128] for h in range(4)], 0)
    wo1 = inp["cd_w_out"][0]
    m["wout"] = np.stack([_pk(rows0), _pk(wo1)])
    w = inp["cd_w_in"][0]
    hc = list(range(r * HC, (r + 1) * HC))
    hd = list(range(r * HD, (r + 1) * HD))
    cols = []
    for base in (0, 512, 1024):
        cols += [w[:, base + h * 64:base + (h + 1) * 64] for h in hc]
    for base in (1544, 2056, 2568):
        cols += [w[:, base + h * 64:base + (h + 1) * 64] for h in hd]
    cols += [w[:, 1536 + h:1536 + h + 1] for h in hc]
    m["w1in"] = _pk(np.concatenate(cols, 1))
    m["bf"] = np.ascontiguousarray(inp["fox_b_f"][0][hc][:, None])
    rt = inp["chunk_rel_table"][0]
    idd = np.clip(q - k, -128, 128) + 128
    idp = np.clip(q + 128 - k, -128, 128) + 128
    m["relb"] = np.ascontiguousarray(np.stack([np.stack([rt[h][idd], rt[h][idp]]) for h in hd]))
    m["relc"] = np.ascontiguousarray(np.broadcast_to(rt[hd, 256][None, :], (128, HD)))
    def pk_fc(wm):
        return np.ascontiguousarray(wm.reshape(2, 8, 128, NFC, 128).transpose(0, 3, 2, 1, 4).reshape(2, NFC, 128, 8 * 128))
    m["wg"] = pk_fc(inp["ffn_w_gate"])
    m["wv"] = pk_fc(inp["ffn_w_val"])
    m["wd"] = np.stack([_pk(inp["ffn_w_down"][l]) for l in range(2)])
    cw = inp["ffn_conv_w"].reshape(2, 3, NFC, 128)
    m["convw"] = np.ascontiguousarray(cw.transpose(3, 0, 1, 2).reshape(128, 2 * 3 * NFC))
    m["convb"] = np.ascontiguousarray(inp["ffn_conv_b"].reshape(2, NFC, 128).transpose(2, 0, 1).reshape(128, 2 * NFC))
    return {kk: np.ascontiguousarray(vv, dtype=f) for kk, vv in m.items()}


_CACHE = {}


def kernel(**inputs):
    inp = {k_: np.asarray(v) for k_, v in inputs.items()}
    if "nc" not in _CACHE:
        _CACHE["nc"] = build()
    nc, kb = _CACHE["nc"]
    in_maps = [prep_inputs(inp, c) for c in range(NCORES)]
    res = run_bass_kernel_spmd(nc, in_maps, core_ids=list(range(NCORES)))
    out = np.empty((4, S, D), np.float32)
    for c in range(NCORES):
        b, r = c // PS, c % PS
        o = res.results[c]["outT"]
        out[b, r * OWN:(r + 1) * OWN, :] = o.transpose(2, 1, 0).reshape(OWN, D)
    return out
```

```python
import contextlib
import math
import numpy as np
import concourse.bass as bass
import concourse.mybir as mybir
from concourse.bass_utils import run_bass_kernel_spmd

F32 = mybir.dt.float32
BF16 = mybir.dt.bfloat16
AF = mybir.ActivationFunctionType
ALU = mybir.AluOpType

PS = 1
NCORES = 4 * PS
S = 8192
D = 1024
GW = 512
NG = S // GW
OWN = S // PS
NGO = OWN // GW
HA, HB, HC, HD = 8 // PS, 4 // PS, 8 // PS, 8 // PS
DFF = 2816
NFC = DFF // 128
ALPHA = 4 ** 0.25
EPS_LN = 1e-5
EPS_RMS = 1e-6
SCALE_A = 96 ** -0.5
SCALE_B = 0.125
LAM_INIT = 0.8 - 0.6 * math.exp(-0.3 * 0)
ROWS0 = HA * 64 + HB * 128
ROWS1 = HC * 64 + HD * 64


class Buf:
    __slots__ = ("writers", "dma_writers", "readers", "dma_readers")

    def __init__(self):
        self.writers = {}
        self.dma_writers = []
        self.readers = {}
        self.dma_readers = []


class Op:
    __slots__ = ("eng", "fn", "deps", "is_dma", "sem", "val", "signals", "pre", "cc")

    def __init__(self, eng, fn, is_dma, cc):
        self.eng = eng
        self.fn = fn
        self.is_dma = is_dma
        self.cc = cc
        self.deps = set()
        self.sem = None
        self.val = 0
        self.signals = False
        self.pre = None


class T:
    __slots__ = ("ap", "b")

    def __init__(self, ap, b=None):
        self.ap = ap
        self.b = b if b is not None else Buf()

    def __getitem__(self, idx):
        return T(self.ap[idx], self.b)


ENGS = ["pe", "act", "dve", "pool", "sp"]
COMPUTE = ("pe", "act", "dve", "pool")


class Prog:
    NDMA_SEMS = 8

    def __init__(self, nc):
        self.nc = nc
        self.ops = []
        self.stack = contextlib.ExitStack()
        self.last = {}
        self.pending_dma = []

    def op(self, eng, fn, reads=(), writes=(), dma=False, cc=False, after=()):
        asyn = dma or cc
        o = Op(eng, fn, asyn, cc)
        for x in after:
            if x is not None:
                o.deps.add(x)
        for t in reads:
            b = t.b
            o.deps.update(b.writers.values())
            o.deps.update(b.dma_writers)
        for t in writes:
            b = t.b
            if b.readers or b.dma_readers:
                o.deps.update(b.readers.values())
                o.deps.update(b.dma_readers)
                b.readers = {}
                b.dma_readers = []
                b.writers = {}
                b.dma_writers = []
        for t in reads:
            if asyn:
                t.b.dma_readers.append(o)
            else:
                t.b.readers[eng] = o
        for t in writes:
            if asyn:
                t.b.dma_writers.append(o)
            else:
                t.b.writers[eng] = o
        o.deps.discard(o)
        self.ops.append(o)
        if asyn:
            self.pending_dma.append(o)
        else:
            self.last[eng] = o
        return o

    def barrier(self):
        deps = list(self.last.values()) + list(self.pending_dma)
        self.pending_dma = []
        for e in ENGS:
            self.op(e, None, after=deps)

    def emit(self):
        nc = self.nc
        per = {e: [] for e in ENGS}
        for o in self.ops:
            per[o.eng].append(o)
        for o in self.ops:
            for d in o.deps:
                if d.eng == o.eng and o.eng == "pe" and not d.is_dma and not o.is_dma:
                    continue
                d.signals = True
        st = self.stack
        sems = {e: st.enter_context(nc.semaphore("s_" + e)) for e in COMPUTE}
        dma_sems = {}
        for e in ENGS:
            if any(o.is_dma and not o.cc for o in per[e]):
                dma_sems[e] = [st.enter_context(nc.semaphore("d_%s%d" % (e, i))) for i in range(self.NDMA_SEMS)]
        cc_sem = st.enter_context(nc.semaphore("s_cc")) if any(o.cc for o in self.ops) else None
        ccnt = 0
        for e in ENGS:
            cnt = 0
            dcnt = 0
            for o in per[e]:
                if o.cc:
                    o.sem = cc_sem
                    o.pre = (cc_sem, ccnt) if ccnt > 0 else None
                    ccnt += 1
                    o.val = ccnt
                    o.signals = True
                elif o.is_dma:
                    s = dma_sems[e][dcnt % self.NDMA_SEMS]
                    k = dcnt // self.NDMA_SEMS
                    o.sem = s
                    o.val = 16 * (k + 1)
                    o.pre = (s, 16 * k) if k > 0 else None
                    o.signals = True
                    dcnt += 1
                elif o.signals and o.fn is not None:
                    cnt += 1
                    o.sem = sems[e]
                    o.val = cnt
        engobj = {"pe": "tensor", "act": "scalar", "dve": "vector", "pool": "gpsimd", "sp": "sync"}

        def run(e, eo):
            waited = {}
            for o in per[e]:
                need = {}
                for d in o.deps:
                    if d.sem is None:
                        continue
                    if d.eng == e and e == "pe" and not d.is_dma and not o.is_dma:
                        continue
                    key = id(d.sem)
                    if key not in need or need[key][1] < d.val:
                        need[key] = (d.sem, d.val)
                if o.pre is not None:
                    key = id(o.pre[0])
                    if key not in need or need[key][1] < o.pre[1]:
                        need[key] = o.pre
                for key, (s, v) in need.items():
                    if waited.get(key, 0) < v:
                        eo.wait_ge(s, v)
                        waited[key] = v
                if o.fn is None:
                    continue
                ins = o.fn(eo)
                if o.cc:
                    ins.then_inc(o.sem, 1)
                elif o.is_dma:
                    ins.then_inc(o.sem, 16)
                elif o.signals:
                    ins.then_inc(o.sem, 1)

        with nc.Block() as block:
            for e in ENGS:
                if per[e]:
                    getattr(block, engobj[e])(lambda eo, e=e: run(e, eo))


class Rot:
    def __init__(self, tiles):
        self.tiles = tiles
        self.i = 0

    def next(self):
        t = self.tiles[self.i % len(self.tiles)]
        self.i += 1
        return t


class KB:
    ARENA = 51000

    def __init__(self, debug=()):
        self.nc = nc = bass.Bass("TRN2", target_bir_lowering=False)
        self.P = Prog(nc)
        self.debug = set(debug)
        st = self.P.stack
        self.arena = st.enter_context(nc.sbuf_tensor("arena", [128, self.ARENA], F32))
        self.pers = st.enter_context(nc.sbuf_tensor("pers", [128, 2048], F32))
        self.pers_off = 0
        self.off = 0
        self.psum = [T(st.enter_context(nc.psum_tensor("ps%d" % i, [128, 512], F32))[:, :]) for i in range(8)]
        self.ps_i = 0
        self.inputs = {}
        self.outs = []

    def din(self, name, shape):
        ap = self.nc.dram_tensor(name, list(shape), F32, kind="ExternalInput").ap()
        self.inputs[name] = tuple(shape)
        return ap

    def scratch(self, name, shape, dt):
        kind = "ExternalOutput" if name in self.debug else "Internal"
        return self.nc.dram_tensor(name, list(shape), dt, kind=kind).ap()

    def reset(self):
        self.P.barrier()
        self.off = 0

    def alloc(self, shape, dt=F32, persistent=False):
        n = int(np.prod(shape[1:]))
        words = n if dt == F32 else (n + 1) // 2
        if persistent:
            base = self.pers[:, self.pers_off:self.pers_off + words]
            self.pers_off += words
            assert self.pers_off <= 2048
        else:
            base = self.arena[:, self.off:self.off + words]
            self.off += words
            assert self.off <= self.ARENA, "arena overflow %d" % self.off
        if dt != F32:
            base = base.bitcast(dt)[:, 0:n]
        if len(shape) == 3:
            base = base.rearrange("p (a b) -> p a b", b=shape[2])
        elif len(shape) == 4:
            base = base.rearrange("p (a b c) -> p a b c", b=shape[2], c=shape[3])
        if shape[0] < 128:
            base = base[0:shape[0]]
        return T(base)

    def next_ps(self, lo=0, hi=8):
        i = lo + self.ps_i % (hi - lo)
        self.ps_i += 1
        return self.psum[i]

    @staticmethod
    def _rd(*xs):
        return [x for x in xs if isinstance(x, T)]

    @staticmethod
    def _a(x):
        return x.ap if isinstance(x, T) else x

    def dma(self, out, in_, q="sp", reads=(), writes=()):
        oa, ia = self._a(out), self._a(in_)
        return self.P.op(q, lambda e: e.dma_start(out=oa, in_=ia), reads=self._rd(in_) + list(reads),
                         writes=self._rd(out) + list(writes), dma=True)

    def mm(self, ps, pairs, first=True, last=True, extra_reads=()):
        n = len(pairs)
        for i, (l, r) in enumerate(pairs):
            la, ra, pa = l.ap, r.ap, ps.ap
            self.P.op("pe", lambda e, la=la, ra=ra, pa=pa, s=(first and i == 0), t=(last and i == n - 1):
                      e.matmul(pa, la, ra, start=s, stop=t), reads=[l, r], writes=[ps])

    def act(self, out, in_, func, bias=0.0, scale=1.0, eng="act"):
        oa, ia, ba, sa = out.ap, in_.ap, self._a(bias), self._a(scale)
        self.P.op("act", lambda e: e.activation(out=oa, in_=ia, func=func, bias=ba, scale=sa),
                  reads=self._rd(in_, bias, scale), writes=[out])

    def ts(self, out, in0, s1, s2, op0, op1=None, eng="dve"):
        oa, ia, a1, a2 = out.ap, in0.ap, self._a(s1), self._a(s2)
        if op1 is None:
            fn = lambda e: e.tensor_scalar(out=oa, in0=ia, scalar1=a1, scalar2=None, op0=op0)
        else:
            fn = lambda e: e.tensor_scalar(out=oa, in0=ia, scalar1=a1, scalar2=a2, op0=op0, op1=op1)
        self.P.op(eng, fn, reads=self._rd(in0, s1, s2), writes=[out])

    def stt(self, out, in0, s, in1, op0, op1, eng="dve"):
        oa, ia, sa, ib = out.ap, in0.ap, self._a(s), in1.ap
        self.P.op(eng, lambda e: e.scalar_tensor_tensor(out=oa, in0=ia, scalar=sa, in1=ib, op0=op0, op1=op1),
                  reads=self._rd(in0, s, in1), writes=[out])

    def tt(self, out, in0, in1, op, eng="dve"):
        oa, ia, ib = out.ap, in0.ap, in1.ap
        self.P.op(eng, lambda e: e.tensor_tensor(out=oa, in0=ia, in1=ib, op=op), reads=[in0, in1], writes=[out])

    def copy(self, out, in_, eng="dve"):
        oa, ia = out.ap, in_.ap
        if eng == "act":
            self.P.op("act", lambda e: e.copy(out=oa, in_=ia), reads=[in_], writes=[out])
        else:
            self.P.op(eng, lambda e: e.tensor_copy(out=oa, in_=ia), reads=[in_], writes=[out])

    def memset(self, out, v, eng="pool"):
        oa = out.ap
        self.P.op(eng, lambda e: e.memset(oa, v), writes=[out])

    def recip(self, out, in_):
        oa, ia = out.ap, in_.ap
        self.P.op("dve", lambda e: e.reciprocal(out=oa, in_=ia), reads=[in_], writes=[out])

    def load_cast(self, dst, src_ap, n, stage, q="sp", eng="pool", CH=2048):
        for c0 in range(0, n, CH):
            c1 = min(n, c0 + CH)
            sg = stage.next()
            self.dma(sg[:, 0:c1 - c0], src_ap[:, c0:c1], q=q)
            self.copy(dst[:, c0:c1], sg[:, 0:c1 - c0], eng=eng)


def flat(t):
    return T(t.ap.rearrange("p a b -> p (a b)"), t.b)


def build(debug=(), stop=None, skip=(), sel_units=None):
    kb = KB(debug)
    nc, P = kb.nc, kb.P
    xT = kb.din("xT", [128, 8, S])
    xown = kb.din("xown", [128, 8, OWN]) if PS > 1 else xT
    c8 = kb.din("c8", [128, 8])
    adaw = kb.din("adaw", [96, 128, 8 * 128])
    adab = kb.din("adab", [128, 96])
    lng = kb.din("lng", [128, 32])
    lnb = kb.din("lnb", [128, 32])
    rope = kb.din("rope", [128, 2, S])
    masks = kb.din("masks", [128, 3, 128])
    ncq = 256 + 128 + 96 + 96 + 3 * HB * 128
    w0in = kb.din("w0in", [128, 8 * ncq])
    w0uq = kb.din("w0uq", [128, 2 * 2 * HA * 96])
    w0ukv = kb.din("w0ukv", [128, 2 * HA * 64])
    vec0 = kb.din("vec0", [128, 4])
    lam4 = kb.din("lam4", [128, 4 * 64])
    t5b = kb.din("t5b", [HB, 2, 128, 128])
    t5c = kb.din("t5c", [128, HB])
    wout = kb.din("wout", [2, 128, 8 * 1024])
    ncd = 3 * HC * 64 + 3 * HD * 64 + HC
    w1in = kb.din("w1in", [128, 8 * ncd])
    bf = kb.din("bf", [HC, 1])
    relb = kb.din("relb", [HD, 2, 128, 128])
    relc = kb.din("relc", [128, HD])
    wg = kb.din("wg", [2, NFC, 128, 8 * 128])
    wv = kb.din("wv", [2, NFC, 128, 8 * 128])
    wd = kb.din("wd", [2, 128, NFC * 1024])
    convw = kb.din("convw", [128, 2 * 3 * NFC])
    convb = kb.din("convb", [128, 2 * NFC])
    outT = nc.dram_tensor("outT", [128, 8, OWN], F32, kind="ExternalOutput").ap()

    qA = kb.scratch("qA", [HA, 96, S], BF16)
    kA = kb.scratch("kA", [HA, 96, S], BF16)
    vA = kb.scratch("vA", [HA, 128, 64, 64], BF16)
    qB = kb.scratch("qB", [HB, 128, S], BF16)
    kBt = kb.scratch("kB", [HB, 128, S], BF16)
    vB = kb.scratch("vB", [HB, 128, 64, 128], BF16)
    odraw = kb.scratch("odraw", [HB, 2, 128, S], F32)
    oT0 = kb.scratch("oT0", [ROWS0, S], BF16)
    oT1 = kb.scratch("oT1", [ROWS1, S], BF16)
    x1T = kb.scratch("x1T", [128, 8, OWN], F32)
    u1T = kb.scratch("u1T", [128, 8, OWN], BF16)
    qC = kb.scratch("qC", [HC, 70, S], BF16)
    kC = kb.scratch("kC", [HC, 70, S], BF16)
    vC = kb.scratch("vC", [HC, 128, 64, 64], BF16)
    qDt = kb.scratch("qD", [HD, 64, S], BF16)
    kDt = kb.scratch("kD", [HD, 64, S], BF16)
    vDt = kb.scratch("vD", [HD, 128, 64, 64], BF16)
    wgb = kb.scratch("wgb", [2, NFC, 128, 8 * 128], BF16)
    wvb = kb.scratch("wvb", [2, NFC, 128, 8 * 128], BF16)
    zd = [T(kb.scratch("zd%d" % i, [1, GW], F32)) for i in range(4)]
    zrot = Rot(zd)

    mods = kb.alloc([128, 96], persistent=True)
    sc1 = kb.alloc([128, 96], persistent=True)
    lng_t = kb.alloc([128, 32], persistent=True)
    lnb_t = kb.alloc([128, 32], persistent=True)
    ones_f = kb.alloc([128, 128], persistent=True)
    vec0_t = kb.alloc([128, 4], persistent=True)
    t5c_t = kb.alloc([128, HB], persistent=True)
    relc_t = kb.alloc([128, HD], persistent=True)
    lamc = kb.alloc([128, 4], persistent=True)
    convw_t = kb.alloc([128, 2 * 3 * NFC], persistent=True)
    convb_t = kb.alloc([128, 2 * NFC], persistent=True)
    bf_t = kb.alloc([128, 1], persistent=True)
    carry = kb.alloc([128, 1], persistent=True)
    mask_b = kb.alloc([128, 3, 128], BF16, persistent=True)
    ones_b = kb.alloc([128, GW], BF16, persistent=True)
    ghalo = kb.alloc([128, NFC, 2], persistent=True)

    def mcol(m, which, kc):
        c = m * 24 + which * 8 + kc
        src = mods if which == 0 else sc1
        return src[:, c:c + 1]

    for dst, src in ((lng_t, lng), (lnb_t, lnb), (vec0_t, vec0), (t5c_t, t5c), (relc_t, relc),
                     (convw_t, convw), (convb_t, convb)):
        kb.dma(dst, src)
    kb.dma(bf_t[0:HC, :], bf)
    kb.memset(ones_f, 1.0)
    kb.memset(ones_b, 1.0)
    kb.memset(ghalo, 0.0)
    kb.memset(carry, 0.0)
    cond = kb.alloc([128, 8])
    adab_t = kb.alloc([128, 96])
    mstage = kb.alloc([128, 3, 128])
    lam_t = kb.alloc([128, 4, 64])
    lam_p = kb.alloc([128, 2, 64])
    lam_s = kb.alloc([128, 2])
    kb.dma(cond, c8)
    kb.dma(adab_t, adab)
    kb.dma(mstage, masks)
    kb.dma(flat(lam_t), lam4)
    kb.copy(mask_b, mstage, eng="pool")
    kb.act(cond, cond, AF.Silu)
    kb.tt(lam_p[:, 0, :], lam_t[:, 0, :], lam_t[:, 1, :], ALU.mult)
    kb.tt(lam_p[:, 1, :], lam_t[:, 2, :], lam_t[:, 3, :], ALU.mult)
    for i in range(2):
        oa, ia = lam_s[:, i:i + 1].ap, lam_p[:, i, :].ap
        P.op("dve", lambda e, oa=oa, ia=ia: e.reduce_sum(out=oa, in_=ia, axis=mybir.AxisListType.X),
             reads=[lam_p], writes=[lam_s])
    kb.act(lam_s, lam_s, AF.Exp)
    kb.tt(lamc[:, 0:1], lam_s[:, 1:2], lam_s[:, 0:1], ALU.subtract)
    kb.ts(lamc[:, 0:1], lamc[:, 0:1], -LAM_INIT, None, ALU.add)
    kb.ts(lamc[:, 1:2], vec0_t[:, 3:4], 1.0 - LAM_INIT, None, ALU.mult)
    wrot = Rot([kb.alloc([128, 8, 128]) for _ in range(3)])
    psm = kb.next_ps()
    for col in range(96):
        wt = wrot.next()
        kb.dma(flat(wt), adaw[col])
        kb.mm(psm[:, col:col + 1], [(wt[:, ic, :], cond[:, ic:ic + 1]) for ic in range(8)])
    kb.tt(mods, psm[:, 0:96], adab_t, ALU.add)
    kb.ts(sc1, mods, 1.0, None, ALU.add)

    def cast_ffn_weights(layer):
        stg = Rot([kb.alloc([128, 1024]) for _ in range(2)])
        stb = Rot([kb.alloc([128, 1024], BF16) for _ in range(2)])
        for fc in range(NFC):
            for src, dst in ((wg, wgb), (wv, wvb)):
                a = stg.next()
                b_ = stb.next()
                kb.dma(a, src[layer, fc], q="pool")
                kb.copy(b_, a, eng="pool")
                kb.dma(dst[layer, fc], b_, q="pool")

    def attention_phase(units):
        kb.reset()
        kt = [kb.alloc([128, S], BF16) for _ in range(2)]
        vt = [kb.alloc([128, 64, 129], BF16) for _ in range(2)]
        qg = Rot([kb.alloc([128, GW], BF16) for _ in range(3)])
        pt = Rot([kb.alloc([128, GW], BF16) for _ in range(5)])
        tmp = Rot([kb.alloc([128, GW]) for _ in range(3)])
        rz = Rot([kb.alloc([128, GW]) for _ in range(2)])
        bc = Rot([kb.alloc([64, GW]) for _ in range(2)])
        ot = Rot([kb.alloc([64, GW]) for _ in range(4)])
        btiles = [[kb.alloc([128, 128]) for _ in range(2)] for _ in range(2)]
        for v in vt:
            kb.memset(v[:, :, 64:65], 1.0)

        def load_unit(ui):
            u = units[ui]
            R = u["R"]
            k_, v_ = kt[ui % 2], vt[ui % 2]
            for c in range(4):
                kb.dma(k_[0:R, c * 2048:(c + 1) * 2048], u["kT"][:, c * 2048:(c + 1) * 2048])
            kb.dma(v_[:, :, 0:64], u["vsrc"][0])
            if len(u["vsrc"]) > 1:
                kb.dma(v_[:, :, 65:129], u["vsrc"][1])
            if u["bias"] is not None:
                kb.dma(btiles[ui % 2][0], u["bias"][0])
                kb.dma(btiles[ui % 2][1], u["bias"][1])

        def load_q(ui, G):
            u = units[ui]
            t = qg.next()
            kb.dma(t[0:u["R"], :], u["qT"][:, G * GW:(G + 1) * GW])
            return t

        seq = [(ui, G) for ui in range(len(units)) for G in range(NG)]

        def make_plan(kind, G):
            plan = []
            if kind in ("cc", "fc"):
                for kbi in range(4 * G):
                    plan.append((kbi, 0, GW, None))
                for j in range(4):
                    plan.append((4 * G + j, 128 * j, GW, (128 * j, 0 if kind == "cc" else 1)))
            else:
                for r in (-1, -2, -3, -4):
                    if 4 * G + r >= 0:
                        c1 = 64 * (2 * r + 10)
                        plan.append((4 * G + r, 0, c1, (c1 - 128, 2)))
                for r in range(4):
                    plan.append((4 * G + r, 128 * r, GW, (128 * r, 0)))
            return plan

        jobs = []
        for si, (ui, G) in enumerate(seq):
            plan = make_plan(units[ui]["kind"], G)
            for pi, it in enumerate(plan):
                jobs.append((si, ui, G, pi, len(plan), it))
        load_unit(0)
        qn = {0: load_q(0, 0)}
        ptile = {}
        LA = 2

        def front(j):
            si, ui, G, pi, n, (kbi, c0, c1, mk) = jobs[j]
            u = units[ui]
            R, scale = u["R"], u["scale"]
            k_ = kt[ui % 2]
            if pi == 0 and si + 1 < len(seq):
                qn[si + 1] = load_q(*seq[si + 1])
            qcur = qn[si]
            s = kb.next_ps(0, 4)
            kb.mm(s[:, c0:c1], [(k_[0:R, kbi * 128:(kbi + 1) * 128], qcur[0:R, c0:c1])])
            p = pt.next()
            ptile[j] = p
            if u["bias"] is not None:
                bt = btiles[ui % 2]
                cb = u["bias"][2]
                runs = []
                cur = None
                for cc0 in range(c0, c1, 128):
                    dlt = (4 * G + cc0 // 128) - kbi
                    if dlt in (0, 1):
                        tm = tmp.next()
                        kb.stt(tm[:, cc0:cc0 + 128], s[:, cc0:cc0 + 128], scale, bt[dlt], ALU.mult, ALU.add)
                        kb.act(p[:, cc0:cc0 + 128], tm[:, cc0:cc0 + 128], AF.Exp)
                        cur = None
                    else:
                        if cur is None:
                            cur = [cc0, cc0 + 128]
                            runs.append(cur)
                        else:
                            cur[1] = cc0 + 128
                for a0, a1 in runs:
                    kb.act(p[:, a0:a1], s[:, a0:a1], AF.Exp, bias=cb, scale=scale)
            else:
                kb.act(p[:, c0:c1], s[:, c0:c1], AF.Exp, scale=scale)
            if mk is not None:
                m0, mi = mk
                kb.tt(p[:, m0:m0 + 128], p[:, m0:m0 + 128], mask_b[:, mi, :], ALU.mult, eng="pool")

        def back(j):
            si, ui, G, pi, n, (kbi, c0, c1, mk) = jobs[j]
            u = units[ui]
            nv = len(u["vsrc"])
            v_ = vt[ui % 2]
            if pi == 0 and G == 0 and ui + 1 < len(units):
                load_unit(ui + 1)
            p = ptile.pop(j)
            accs = [kb.psum[4 + 2 * (si % 2) + v] for v in range(nv)]
            for v in range(nv):
                vc0, vc1 = (0, 65) if v == 0 else (65, 129)
                kb.mm(accs[v][0:vc1 - vc0, c0:c1], [(v_[:, kbi, vc0:vc1], p[:, c0:c1])],
                      first=(pi == 0), last=(pi == n - 1))
            if pi < n - 1:
                return
            rzt = rz.next()
            kb.recip(rzt[64:65, :], accs[0][64:65, :])
            zslot = zrot.next()
            kb.dma(zslot, rzt[64:65, :])
            bct = bc.next()
            kb.dma(bct, T(zslot.ap.broadcast_to([64, GW]), zslot.b))
            for v in range(nv):
                o_ = ot.next()
                if u["odt"] == BF16:
                    ob = T(o_.ap.bitcast(BF16)[:, 0:GW], o_.b)
                else:
                    ob = o_
                kb.tt(ob, accs[v][0:64, :], bct, ALU.mult)
                kb.dma(u["outs"][v][:, G * GW:(G + 1) * GW], ob)

        for t in range(len(jobs) + LA):
            if t < len(jobs):
                front(t)
            if t - LA >= 0:
                back(t - LA)

    def phase_a0():
        kb.reset()
        cast_stage = Rot([kb.alloc([128, 2048]) for _ in range(2)])
        W = kb.alloc([128, 8, ncq], BF16)
        Wuq = kb.alloc([128, 2, 2 * HA * 96], BF16)
        Wukv = kb.alloc([128, 2 * HA * 64], BF16)
        kb.load_cast(flat(W), w0in, 8 * ncq, cast_stage)
        kb.load_cast(flat(Wuq), w0uq, 2 * 2 * HA * 96, cast_stage)
        kb.load_cast(Wukv, w0ukv, 2 * HA * 64, cast_stage)
        o_cq, o_ckv, o_kr, o_krs = 0, 256, 384, 480
        o_dq, o_dk, o_dv = 576, 576 + HB * 128, 576 + 2 * HB * 128
        xg = Rot([kb.alloc([128, 8, GW]) for _ in range(2)])
        rp = Rot([kb.alloc([128, 2, GW]) for _ in range(2)])
        ug = Rot([kb.alloc([128, 8, GW], BF16) for _ in range(2)])
        cq_sb = kb.alloc([128, 2, GW])
        sq = Rot([kb.alloc([128, GW]) for _ in range(2)])
        rs = Rot([kb.alloc([128, GW]) for _ in range(2)])
        cqn = kb.alloc([128, 2, GW], BF16)
        ckv_sb = kb.alloc([128, GW])
        ckvn = kb.alloc([128, GW], BF16)
        qh = Rot([kb.alloc([128, GW], BF16) for _ in range(3)])
        t1 = Rot([kb.alloc([128, GW]) for _ in range(2)])
        t2 = Rot([kb.alloc([128, GW]) for _ in range(2)])
        krope = kb.alloc([128, GW], BF16)
        kn = Rot([kb.alloc([64, GW], BF16) for _ in range(3)])
        v_sb = kb.alloc([128, 4, HA * 64], BF16)
        dqk = Rot([kb.alloc([128, GW], BF16) for _ in range(3)])
        vd_sb = kb.alloc([128, 4, HB * 128], BF16)

        def loads(g):
            x_ = xg.next()
            r_ = rp.next()
            kb.dma(x_, xT[:, :, g * GW:(g + 1) * GW])
            kb.dma(r_[64:96], rope[64:96, :, g * GW:(g + 1) * GW])
            return x_, r_

        nxt = loads(0)
        for g in range(NG):
            x_, r_ = nxt
            if g + 1 < NG:
                nxt = loads(g + 1)
            cs = slice(g * GW, (g + 1) * GW)
            u_ = ug.next()
            for kc in range(8):
                kb.ts(u_[:, kc, :], x_[:, kc, :], mcol(0, 1, kc), mcol(0, 0, kc), ALU.mult, ALU.add)
            pss = kb.next_ps()
            for mc in range(2):
                ps = kb.next_ps()
                kb.mm(ps, [(W[:, kc, o_cq + mc * 128:o_cq + (mc + 1) * 128], u_[:, kc, :]) for kc in range(8)])
                kb.copy(cq_sb[:, mc, :], ps, eng="act")
                s_ = sq.next()
                kb.act(s_, ps, AF.Square)
                kb.mm(pss, [(ones_f, s_)], first=(mc == 0), last=(mc == 1))
            r1 = rs.next()
            kb.act(r1, pss, AF.Sqrt, bias=EPS_RMS, scale=1.0 / 256)
            kb.recip(r1, r1)
            for mc in range(2):
                kb.stt(cqn[:, mc, :], cq_sb[:, mc, :], vec0_t[:, mc:mc + 1], r1, ALU.mult, ALU.mult)
            ps = kb.next_ps()
            kb.mm(ps, [(W[:, kc, o_ckv:o_ckv + 128], u_[:, kc, :]) for kc in range(8)])
            kb.copy(ckv_sb, ps, eng="act")
            s_ = sq.next()
            kb.act(s_, ps, AF.Square)
            pss = kb.next_ps()
            kb.mm(pss, [(ones_f, s_)])
            r2 = rs.next()
            kb.act(r2, pss, AF.Sqrt, bias=EPS_RMS, scale=1.0 / 128)
            kb.recip(r2, r2)
            kb.stt(ckvn, ckv_sb, vec0_t[:, 2:3], r2, ALU.mult, ALU.mult)
            for h in range(HA):
                ps1 = kb.next_ps()
                ps2 = kb.next_ps()
                kb.mm(ps1[0:96, :], [(Wuq[:, k2, h * 96:(h + 1) * 96], cqn[:, k2, :]) for k2 in range(2)])
                kb.mm(ps2[0:96, :], [(Wuq[:, k2, HA * 96 + h * 96:HA * 96 + (h + 1) * 96], cqn[:, k2, :]) for k2 in range(2)])
                q_ = qh.next()
                a_, b_ = t1.next(), t2.next()
                kb.copy(q_[0:64, :], ps1[0:64, :], eng="act")
                kb.tt(a_[64:96, :], ps1[64:96, :], r_[64:96, 0, :], ALU.mult)
                kb.tt(b_[64:96, :], ps2[64:96, :], r_[64:96, 1, :], ALU.mult)
                kb.tt(q_[64:96, :], a_[64:96, :], b_[64:96, :], ALU.add)
                kb.dma(qA[h][:, cs], q_[0:96, :])
            ps1 = kb.next_ps()
            ps2 = kb.next_ps()
            kb.mm(ps1[0:96, :], [(W[:, kc, o_kr:o_kr + 96], u_[:, kc, :]) for kc in range(8)])
            kb.mm(ps2[0:96, :], [(W[:, kc, o_krs:o_krs + 96], u_[:, kc, :]) for kc in range(8)])
            a_, b_ = t1.next(), t2.next()
            kb.tt(a_[64:96, :], ps1[64:96, :], r_[64:96, 0, :], ALU.mult)
            kb.tt(b_[64:96, :], ps2[64:96, :], r_[64:96, 1, :], ALU.mult)
            kb.tt(krope[64:96, :], a_[64:96, :], b_[64:96, :], ALU.add)
            for h in range(HA):
                kb.dma(kA[h][64:96, cs], krope[64:96, :])
            for h in range(HA):
                ps = kb.next_ps()
                kb.mm(ps[0:64, :], [(Wukv[:, h * 64:(h + 1) * 64], ckvn)])
                k_ = kn.next()
                kb.copy(k_, ps[0:64, :], eng="act")
                kb.dma(kA[h][0:64, cs], k_)
            for blk in range(4):
                ps = kb.next_ps()
                kb.mm(ps[:, 0:HA * 64], [(ckvn[:, blk * 128:(blk + 1) * 128], Wukv[:, HA * 64:2 * HA * 64])])
                kb.copy(v_sb[:, blk, :], ps[:, 0:HA * 64], eng="dve")
            for h in range(HA):
                kb.dma(vA[h][:, 4 * g:4 * g + 4, :], v_sb[:, :, h * 64:(h + 1) * 64])
            for h in range(HB):
                for off, dst in ((o_dq, qB), (o_dk, kBt)):
                    ps = kb.next_ps()
                    kb.mm(ps, [(W[:, kc, off + h * 128:off + (h + 1) * 128], u_[:, kc, :]) for kc in range(8)])
                    t_ = dqk.next()
                    kb.copy(t_, ps, eng="act")
                    kb.dma(dst[h][:, cs], t_)
            for blk in range(4):
                ps = kb.next_ps()
                kb.mm(ps[:, 0:HB * 128], [(u_[:, kc, blk * 128:(blk + 1) * 128], W[:, kc, o_dv:o_dv + HB * 128])
                                          for kc in range(8)])
                kb.copy(vd_sb[:, blk, :], ps[:, 0:HB * 128], eng="dve")
            for h in range(HB):
                kb.dma(vB[h][:, 4 * g:4 * g + 4, :], vd_sb[:, :, h * 128:(h + 1) * 128])

    def phase_c0():
        kb.reset()
        o0 = Rot([kb.alloc([128, GW]) for _ in range(2)])
        o1 = Rot([kb.alloc([128, GW]) for _ in range(2)])
        dd = Rot([kb.alloc([128, GW]) for _ in range(2)])
        sq = Rot([kb.alloc([128, GW]) for _ in range(2)])
        rs = Rot([kb.alloc([128, GW]) for _ in range(2)])
        ob = Rot([kb.alloc([128, GW], BF16) for _ in range(2)])
        for h in range(HB):
            for g in range(NG):
                cs = slice(g * GW, (g + 1) * GW)
                a, b_ = o0.next(), o1.next()
                kb.dma(a, odraw[h, 0][:, cs])
                kb.dma(b_, odraw[h, 1][:, cs])
                d_ = dd.next()
                kb.stt(d_, b_, lamc[:, 0:1], a, ALU.mult, ALU.add)
                s_ = sq.next()
                kb.act(s_, d_, AF.Square)
                ps = kb.next_ps()
                kb.mm(ps, [(ones_f, s_)])
                r_ = rs.next()
                kb.act(r_, ps, AF.Sqrt, bias=EPS_RMS, scale=1.0 / 128)
                kb.recip(r_, r_)
                o_ = ob.next()
                kb.stt(o_, d_, lamc[:, 1:2], r_, ALU.mult, ALU.mult)
                kb.dma(oT0[HA * 64 + h * 128:HA * 64 + (h + 1) * 128, cs], o_)

    def layer_norm(z, m_idx, xout, extra):
        sqr = Rot(extra["sq"])
        psm, psq = kb.next_ps(), kb.next_ps()
        for kc in range(8):
            s_ = sqr.next()
            kb.act(s_, z[:, kc, :], AF.Square)
            kb.mm(psm, [(ones_f, z[:, kc, :])], first=(kc == 0), last=(kc == 7))
            kb.mm(psq, [(ones_f, s_)], first=(kc == 0), last=(kc == 7))
        mean, msq, rstd = extra["st"]
        kb.ts(mean, psm, 1.0 / D, None, ALU.mult)
        kb.tt(msq, mean, mean, ALU.mult)
        kb.stt(rstd, psq, 1.0 / D, msq, ALU.mult, ALU.subtract)
        kb.act(rstd, rstd, AF.Sqrt, bias=EPS_LN)
        kb.recip(rstd, rstd)
        for kc in range(8):
            kb.tt(z[:, kc, :], z[:, kc, :], mean, ALU.subtract)
            kb.tt(z[:, kc, :], z[:, kc, :], rstd, ALU.mult)
            kb.ts(xout[:, kc, :], z[:, kc, :], lng_t[:, m_idx * 8 + kc:m_idx * 8 + kc + 1],
                  lnb_t[:, m_idx * 8 + kc:m_idx * 8 + kc + 1], ALU.mult, ALU.add)

    def phase_b(layer, o_all, rows):
        kb.reset()
        m_mix, m_ffn = 2 * layer, 2 * layer + 1
        cast_stage = Rot([kb.alloc([128, 1024]) for _ in range(2)])
        Wo = kb.alloc([128, 8, 1024], BF16)
        Wd = kb.alloc([128, NFC, 1024], BF16)
        kb.load_cast(flat(Wo), wout[layer], 8 * 1024, cast_stage, CH=1024)
        kb.load_cast(flat(Wd), wd[layer], NFC * 1024, cast_stage, CH=1024)
        wgv = Rot([kb.alloc([128, 2, 8, 128], BF16) for _ in range(3)])
        og = Rot([kb.alloc([128, 8, GW], BF16) for _ in range(2)])
        xr = Rot([kb.alloc([128, 8, GW]) for _ in range(1)])
        z = kb.alloc([128, 8, GW])
        xn = kb.alloc([128, 8, GW])
        u2 = kb.alloc([128, 8, GW], BF16)
        hT = kb.alloc([128, NFC, GW], BF16)
        lnx = {"sq": [kb.alloc([128, GW]) for _ in range(2)], "st": [kb.alloc([128, GW]) for _ in range(3)]}
        gsb = Rot([kb.alloc([128, GW + 2]) for _ in range(2)])
        acc = Rot([kb.alloc([128, GW]) for _ in range(2)])
        sg = Rot([kb.alloc([128, GW]) for _ in range(2)])
        xsrc = xown if layer == 0 else x1T
        kb.memset(ghalo, 0.0)

        def loads(g):
            o_, x_ = og.next(), xr.next()
            cs = slice(g * GW, (g + 1) * GW)
            for kc in range(8):
                kb.dma(o_[:, kc, :], o_all[kc * 128:(kc + 1) * 128, cs])
            kb.dma(x_, xsrc[:, :, cs])
            return o_, x_

        nxt = loads(0)
        for g in range(NGO):
            o_, x_ = nxt
            cs = slice(g * GW, (g + 1) * GW)
            kb.act(flat(x_), flat(x_), AF.Identity, scale=ALPHA)
            for mc in range(8):
                ps = kb.next_ps()
                kb.mm(ps, [(Wo[:, kc, mc * 128:(mc + 1) * 128], o_[:, kc, :]) for kc in range(8)])
                kb.stt(z[:, mc, :], ps, mcol(m_mix, 2, mc), x_[:, mc, :], ALU.mult, ALU.add)
            if g + 1 < NGO:
                nxt = loads(g + 1)
            layer_norm(z, m_mix, xn, lnx)
            for kc in range(8):
                kb.act(u2[:, kc, :], xn[:, kc, :], AF.Identity, bias=mcol(m_ffn, 0, kc), scale=mcol(m_ffn, 1, kc))
            kb.act(flat(xn), flat(xn), AF.Identity, scale=ALPHA)
            for fc in range(NFC):
                w_ = wgv.next()
                kb.dma(flat(w_[:, 0]), wgb[layer, fc])
                kb.dma(flat(w_[:, 1]), wvb[layer, fc])
                psg, psv = kb.next_ps(), kb.next_ps()
                kb.mm(psg, [(w_[:, 0, kc, :], u2[:, kc, :]) for kc in range(8)])
                kb.mm(psv, [(w_[:, 1, kc, :], u2[:, kc, :]) for kc in range(8)])
                g_ = gsb.next()
                kb.copy(g_[:, 0:2], ghalo[:, fc, :], eng="pool")
                kb.copy(g_[:, 2:GW + 2], psg, eng="act")
                kb.copy(ghalo[:, fc, :], g_[:, GW:GW + 2], eng="pool")
                a_ = acc.next()
                cw = lambda j: convw_t[:, (layer * 3 + j) * NFC + fc:(layer * 3 + j) * NFC + fc + 1]
                kb.ts(a_, g_[:, 0:GW], cw(0), convb_t[:, layer * NFC + fc:layer * NFC + fc + 1], ALU.mult, ALU.add)
                kb.stt(a_, g_[:, 1:GW + 1], cw(1), a_, ALU.mult, ALU.add)
                kb.stt(a_, g_[:, 2:GW + 2], cw(2), a_, ALU.mult, ALU.add)
                s_ = sg.next()
                kb.act(s_, a_, AF.Silu)
                kb.tt(hT[:, fc, :], s_, psv, ALU.mult)
            for mc in range(8):
                ps = kb.next_ps()
                kb.mm(ps, [(Wd[:, fc, mc * 128:(mc + 1) * 128], hT[:, fc, :]) for fc in range(NFC)])
                kb.stt(z[:, mc, :], ps, mcol(m_ffn, 2, mc), xn[:, mc, :], ALU.mult, ALU.add)
            layer_norm(z, m_ffn, xn, lnx)
            if layer == 0:
                kb.dma(x1T[:, :, cs], xn)
                u_ = u2
                for kc in range(8):
                    kb.act(u_[:, kc, :], xn[:, kc, :], AF.Identity, bias=mcol(2, 0, kc), scale=mcol(2, 1, kc))
                kb.dma(u1T[:, :, cs], u_)
            else:
                kb.dma(outT[:, :, cs], xn)

    def phase_a1(u_all):
        kb.reset()
        cast_stage = Rot([kb.alloc([128, 2048]) for _ in range(2)])
        W = kb.alloc([128, 8, ncd], BF16)
        kb.load_cast(flat(W), w1in, 8 * ncd, cast_stage)
        o_cq, o_ck, o_cv = 0, HC * 64, 2 * HC * 64
        o_dq, o_dk, o_dv = 3 * HC * 64, 3 * HC * 64 + HD * 64, 3 * HC * 64 + 2 * HD * 64
        o_fl = 3 * HC * 64 + 3 * HD * 64
        ug = Rot([kb.alloc([128, 8, GW], BF16) for _ in range(2)])
        qk = Rot([kb.alloc([128, GW], BF16) for _ in range(3)])
        vc_sb = kb.alloc([128, 4, HC * 64], BF16)
        vd_sb = kb.alloc([128, 4, HD * 64], BF16)
        e_ = kb.alloc([128, GW])
        csA = kb.alloc([128, GW])
        csB = kb.alloc([128, GW])
        f8 = kb.alloc([128, GW])
        r1 = kb.alloc([128, GW])
        fp_ = Rot([kb.alloc([128, 6, GW], BF16) for _ in range(2)])
        kb.memset(carry, 0.0)

        def loads(g):
            u_ = ug.next()
            kb.dma(u_, u_all[:, :, g * GW:(g + 1) * GW])
            return u_

        nxt = loads(0)
        for g in range(NG):
            u_ = nxt
            if g + 1 < NG:
                nxt = loads(g + 1)
            cs = slice(g * GW, (g + 1) * GW)
            for off, dst, nh in ((o_cq, qC, HC), (o_ck, kC, HC), (o_dq, qDt, HD), (o_dk, kDt, HD)):
                for hp in range(nh // 2):
                    ps = kb.next_ps()
                    kb.mm(ps, [(W[:, kc, off + hp * 128:off + (hp + 1) * 128], u_[:, kc, :]) for kc in range(8)])
                    t_ = qk.next()
                    kb.copy(t_, ps, eng="act")
                    kb.dma(dst[2 * hp][0:64, cs], t_[0:64, :])
                    kb.dma(dst[2 * hp + 1][0:64, cs], t_[64:128, :])
            for off, sb, dst, nh in ((o_cv, vc_sb, vC, HC), (o_dv, vd_sb, vDt, HD)):
                for blk in range(4):
                    ps = kb.next_ps()
                    kb.mm(ps[:, 0:nh * 64], [(u_[:, kc, blk * 128:(blk + 1) * 128], W[:, kc, off:off + nh * 64])
                                             for kc in range(8)])
                    kb.copy(sb[:, blk, :], ps[:, 0:nh * 64], eng="dve")
                for h in range(nh):
                    kb.dma(dst[h][:, 4 * g:4 * g + 4, :], sb[:, :, h * 64:(h + 1) * 64])
            ps = kb.next_ps()
            kb.mm(ps[0:HC, :], [(W[:, kc, o_fl:o_fl + HC], u_[:, kc, :]) for kc in range(8)])
            kb.ts(e_[0:HC, :], ps[0:HC, :], bf_t[0:HC, :], -1.0, ALU.add, ALU.mult)
            kb.act(e_[0:HC, :], e_[0:HC, :], AF.Exp)
            kb.act(csA[0:HC, :], e_[0:HC, :], AF.Ln, bias=1.0)
            a, b_ = csA, csB
            sft = 1
            while sft < GW:
                kb.copy(b_[0:HC, 0:sft], a[0:HC, 0:sft], eng="dve")
                kb.tt(b_[0:HC, sft:GW], a[0:HC, sft:GW], a[0:HC, 0:GW - sft], ALU.add)
                a, b_ = b_, a
                sft *= 2
            kb.ts(f8[0:HC, :], a[0:HC, :], carry[0:HC, :], -8.0, ALU.add, ALU.mult)
            kb.tt(carry[0:HC, :], carry[0:HC, :], a[0:HC, GW - 1:GW], ALU.add)
            fp = fp_.next()
            kb.copy(fp[0:HC, 0, :], f8[0:HC, :], eng="dve")
            kb.tt(r1[0:HC, :], f8[0:HC, :], fp[0:HC, 0, :], ALU.subtract)
            kb.copy(fp[0:HC, 1, :], r1[0:HC, :], eng="dve")
            kb.tt(r1[0:HC, :], r1[0:HC, :], fp[0:HC, 1, :], ALU.subtract)
            kb.copy(fp[0:HC, 2, :], r1[0:HC, :], eng="dve")
            for j in range(3):
                kb.ts(fp[0:HC, 3 + j, :], fp[0:HC, j, :], -1.0, None, ALU.mult)
            for h in range(HC):
                kb.dma(qC[h:h + 1, 64:67, cs], fp[h:h + 1, 0:3, :])
                kb.dma(kC[h:h + 1, 67:70, cs], fp[h:h + 1, 3:6, :])
                kb.dma(qC[h][67:70, cs], ones_b[0:3, :])
                kb.dma(kC[h][64:67, cs], ones_b[0:3, :])

    t5_tiles = [(T(t5b[h, 0]), T(t5b[h, 1])) for h in range(HB)]
    rel_tiles = [(T(relb[h, 0]), T(relb[h, 1])) for h in range(HD)]
    units0 = []
    for h in range(HA):
        units0.append(dict(qT=qA[h], kT=kA[h], R=96, vsrc=[vA[h]], scale=SCALE_A, kind="cc", bias=None,
                           outs=[oT0[h * 64:(h + 1) * 64, :]], odt=BF16))
    for h in range(HB):
        for c in range(2):
            units0.append(dict(qT=qB[h][64 * c:64 * c + 64, :], kT=kBt[h][64 * c:64 * c + 64, :], R=64,
                               vsrc=[vB[h][:, :, 0:64], vB[h][:, :, 64:128]], scale=SCALE_B, kind="cc",
                               bias=(t5_tiles[h][0], t5_tiles[h][1], t5c_t[:, h:h + 1]),
                               outs=[odraw[h, c][0:64, :], odraw[h, c][64:128, :]], odt=F32))
    units1 = []
    for h in range(HC):
        units1.append(dict(qT=qC[h], kT=kC[h], R=70, vsrc=[vC[h]], scale=SCALE_B, kind="fc", bias=None,
                           outs=[oT1[h * 64:(h + 1) * 64, :]], odt=BF16))
    for h in range(HD):
        units1.append(dict(qT=qDt[h], kT=kDt[h], R=64, vsrc=[vDt[h]], scale=SCALE_B, kind="band",
                           bias=(rel_tiles[h][0], rel_tiles[h][1], relc_t[:, h:h + 1]),
                           outs=[oT1[HC * 64 + h * 64:HC * 64 + (h + 1) * 64, :]], odt=BF16))
    if sel_units is not None:
        units0 = [units0[i] for i in sel_units[0]]
        units1 = [units1[i] for i in sel_units[1]]
    steps = [("a0", phase_a0), ("cast", lambda: (cast_ffn_weights(0), cast_ffn_weights(1))),
             ("att0", lambda: attention_phase(units0)), ("c0", phase_c0), ("b0", lambda: phase_b(0, oT0, ROWS0)),
             ("a1", lambda: phase_a1(u1T)), ("att1", lambda: attention_phase(units1)),
             ("b1", lambda: phase_b(1, oT1, ROWS1))]
    for name, fn in steps:
        if skip and name in skip:
            continue
        fn()
        if stop == name:
            break
    kb.reset()
    P.emit()
    P.stack.close()
    return nc, kb


def _t5_bucket(rel):
    half, max_exact = 16, 8
    n = np.abs(rel)
    large = max_exact + (np.log(np.maximum(n, 1).astype(np.float32) / max_exact)
                         / math.log(128 / max_exact) * (half - max_exact)).astype(np.int32)
    large = np.minimum(large, half - 1)
    return np.where(rel > 0, half, 0) + np.where(n < max_exact, n, large)


def _pk(w):
    K, N = w.shape
    return np.ascontiguousarray(w.reshape(K // 128, 128, N).transpose(1, 0, 2).reshape(128, (K // 128) * N))


def _col8(v):
    return np.ascontiguousarray(v.reshape(8, 128).T)


def prep_inputs(inp, core):
    f = np.float32
    b, r = core // PS, core % PS
    x = inp["x"][b]
    m = {}
    xTa = np.ascontiguousarray(x.T.reshape(8, 128, S).transpose(1, 0, 2))
    m["xT"] = xTa
    if PS > 1:
        m["xown"] = np.ascontiguousarray(xTa[:, :, r * OWN:(r + 1) * OWN])
    m["c8"] = _col8(inp["c"][b])
    aw = inp["ada_w"].reshape(4, 8, 128, 24, 128)
    m["adaw"] = np.ascontiguousarray(aw.transpose(0, 3, 2, 1, 4).reshape(96, 128, 8 * 128))
    ab = inp["ada_b"].reshape(4, 24, 128)
    m["adab"] = np.ascontiguousarray(ab.transpose(2, 0, 1).reshape(128, 96))
    m["lng"] = np.ascontiguousarray(inp["ln_g"].reshape(4, 8, 128).transpose(2, 0, 1).reshape(128, 32))
    m["lnb"] = np.ascontiguousarray(inp["ln_b"].reshape(4, 8, 128).transpose(2, 0, 1).reshape(128, 32))
    half = 16
    inv = np.power(np.float32(10000.0), -np.arange(half, dtype=f) / half).astype(f)
    ang = np.arange(S, dtype=f)[:, None] * inv[None, :]
    cos, sin = np.cos(ang).astype(f).T, np.sin(ang).astype(f).T
    rp = np.zeros((128, 2, S), f)
    rp[64:96, 0] = np.concatenate([cos, cos], 0)
    rp[64:96, 1] = np.concatenate([-sin, sin], 0)
    m["rope"] = rp
    k = np.arange(128)[:, None]
    q = np.arange(128)[None, :]
    mk = np.ones((128, 3, 128), f)
    mk[:, 0] = np.where((k >= 64) & (q < 64), 0.0, 1.0)
    mk[:, 1] = np.where(k <= q, 1.0, 0.0)
    mk[:, 2] = np.where((k < 64) & (q >= 64), 0.0, 1.0)
    m["masks"] = mk
    w = inp["ab_w_in"][0]
    ha = list(range(r * HA, (r + 1) * HA))
    hb = list(range(r * HB, (r + 1) * HB))
    z64 = np.zeros((D, 64), f)
    kr = w[:, 384:416]
    cols = [w[:, 0:256], w[:, 256:384], z64, kr, z64, kr[:, 16:32], kr[:, 0:16]]
    for base in (416, 928, 1440):
        cols += [w[:, base + h * 128:base + (h + 1) * 128] for h in hb]
    m["w0in"] = _pk(np.concatenate(cols, 1))
    uq = inp["mla_w_uq"][0]
    a1 = [uq[:, h * 96:(h + 1) * 96] for h in ha]
    a2 = [np.concatenate([uq[:, h * 96:h * 96 + 64], uq[:, h * 96 + 80:h * 96 + 96], uq[:, h * 96 + 64:h * 96 + 80]], 1)
          for h in ha]
    m["w0uq"] = _pk(np.concatenate(a1 + a2, 1))
    ukv = inp["mla_w_ukv"][0]
    m["w0ukv"] = _pk(np.concatenate([ukv[:, h * 128:h * 128 + 64] for h in ha] +
                                    [ukv[:, h * 128 + 64:h * 128 + 128] for h in ha], 1))
    v0 = np.zeros((128, 4), f)
    v0[:, 0] = inp["mla_q_norm"][0, 0:128]
    v0[:, 1] = inp["mla_q_norm"][0, 128:256]
    v0[:, 2] = inp["mla_kv_norm"][0]
    v0[:, 3] = inp["diff_sub_g"][0]
    m["vec0"] = v0
    lam = np.concatenate([inp["diff_lq1"][0], inp["diff_lk1"][0], inp["diff_lq2"][0], inp["diff_lk2"][0]])
    m["lam4"] = np.ascontiguousarray(np.broadcast_to(lam[None, :], (128, 256)))
    t5 = inp["t5_table"]
    bd = _t5_bucket(k - q)
    bp = _t5_bucket((k - 128) - q)
    m["t5b"] = np.ascontiguousarray(np.stack([np.stack([t5[h][bd], t5[h][bp]]) for h in hb]))
    m["t5c"] = np.ascontiguousarray(np.broadcast_to(t5[hb, 15][None, :], (128, HB)))
    wo = inp["ab_w_out"][0]
    rows0 = np.concatenate([wo[h * 64:(h + 1) * 64] for h in range(8)] + [wo[512 + h * 128:512 + (h + 1) * 128] for h in range(4)], 0)
    wo1 = inp["cd_w_out"][0]
    m["wout"] = np.stack([_pk(rows0), _pk(wo1)])
    w = inp["cd_w_in"][0]
    hc = list(range(r * HC, (r + 1) * HC))
    hd = list(range(r * HD, (r + 1) * HD))
    cols = []
    for base in (0, 512, 1024):
        cols += [w[:, base + h * 64:base + (h + 1) * 64] for h in hc]
    for base in (1544, 2056, 2568):
        cols += [w[:, base + h * 64:base + (h + 1) * 64] for h in hd]
    cols += [w[:, 1536 + h:1536 + h + 1] for h in hc]
    m["w1in"] = _pk(np.concatenate(cols, 1))
    m["bf"] = np.ascontiguousarray(inp["fox_b_f"][0][hc][:, None])
    rt = inp["chunk_rel_table"][0]
    idd = np.clip(q - k, -128, 128) + 128
    idp = np.clip(q + 128 - k, -128, 128) + 128
    m["relb"] = np.ascontiguousarray(np.stack([np.stack([rt[h][idd], rt[h][idp]]) for h in hd]))
    m["relc"] = np.ascontiguousarray(np.broadcast_to(rt[hd, 256][None, :], (128, HD)))
    def pk_fc(wm):
        return np.ascontiguousarray(wm.reshape(2, 8, 128, NFC, 128).transpose(0, 3, 2, 1, 4).reshape(2, NFC, 128, 8 * 128))
    m["wg"] = pk_fc(inp["ffn_w_gate"])
    m["wv"] = pk_fc(inp["ffn_w_val"])
    m["wd"] = np.stack([_pk(inp["ffn_w_down"][l]) for l in range(2)])
    cw = inp["ffn_conv_w"].reshape(2, 3, NFC, 128)
    m["convw"] = np.ascontiguousarray(cw.transpose(3, 0, 1, 2).reshape(128, 2 * 3 * NFC))
    m["convb"] = np.ascontiguousarray(inp["ffn_conv_b"].reshape(2, NFC, 128).transpose(2, 0, 1).reshape(128, 2 * NFC))
    return {kk: np.ascontiguousarray(vv, dtype=f) for kk, vv in m.items()}


_CACHE = {}


def kernel(**inputs):
    inp = {k_: np.asarray(v) for k_, v in inputs.items()}
    if "nc" not in _CACHE:
        _CACHE["nc"] = build()
    nc, kb = _CACHE["nc"]
    in_maps = [prep_inputs(inp, c) for c in range(NCORES)]
    res = run_bass_kernel_spmd(nc, in_maps, core_ids=list(range(NCORES)))
    out = np.empty((4, S, D), np.float32)
    for c in range(NCORES):
        b, r = c // PS, c % PS
        o = res.results[c]["outT"]
        out[b, r * OWN:(r + 1) * OWN, :] = o.transpose(2, 1, 0).reshape(OWN, D)
    return out
```

```python
import contextlib
import math
import numpy as np
import concourse.bass as bass
import concourse.mybir as mybir
from concourse.bass_utils import run_bass_kernel_spmd

F32 = mybir.dt.float32
BF16 = mybir.dt.bfloat16
AF = mybir.ActivationFunctionType
ALU = mybir.AluOpType

PS = 1
NCORES = 4 * PS
S = 8192
D = 1024
GW = 512
NG = S // GW
OWN = S // PS
NGO = OWN // GW
HA, HB, HC, HD = 8 // PS, 4 // PS, 8 // PS, 8 // PS
DFF = 2816
NFC = DFF // 128
ALPHA = 4 ** 0.25
EPS_LN = 1e-5
EPS_RMS = 1e-6
SCALE_A = 96 ** -0.5
SCALE_B = 0.125
LAM_INIT = 0.8 - 0.6 * math.exp(-0.3 * 0)
ROWS0 = HA * 64 + HB * 128
ROWS1 = HC * 64 + HD * 64


class Buf:
    __slots__ = ("writers", "dma_writers", "readers", "dma_readers")

    def __init__(self):
        self.writers = {}
        self.dma_writers = []
        self.readers = {}
        self.dma_readers = []


class Op:
    __slots__ = ("eng", "fn", "deps", "is_dma", "sem", "val", "signals", "pre", "cc")

    def __init__(self, eng, fn, is_dma, cc):
        self.eng = eng
        self.fn = fn
        self.is_dma = is_dma
        self.cc = cc
        self.deps = set()
        self.sem = None
        self.val = 0
        self.signals = False
        self.pre = None


class T:
    __slots__ = ("ap", "b")

    def __init__(self, ap, b=None):
        self.ap = ap
        self.b = b if b is not None else Buf()

    def __getitem__(self, idx):
        return T(self.ap[idx], self.b)


ENGS = ["pe", "act", "dve", "pool", "sp"]
COMPUTE = ("pe", "act", "dve", "pool")


class Prog:
    NDMA_SEMS = 8

    def __init__(self, nc):
        self.nc = nc
        self.ops = []
        self.stack = contextlib.ExitStack()
        self.last = {}
        self.pending_dma = []

    def op(self, eng, fn, reads=(), writes=(), dma=False, cc=False, after=()):
        asyn = dma or cc
        o = Op(eng, fn, asyn, cc)
        for x in after:
            if x is not None:
                o.deps.add(x)
        for t in reads:
            b = t.b
            o.deps.update(b.writers.values())
            o.deps.update(b.dma_writers)
        for t in writes:
            b = t.b
            if b.readers or b.dma_readers:
                o.deps.update(b.readers.values())
                o.deps.update(b.dma_readers)
                b.readers = {}
                b.dma_readers = []
                b.writers = {}
                b.dma_writers = []
        for t in reads:
            if asyn:
                t.b.dma_readers.append(o)
            else:
                t.b.readers[eng] = o
        for t in writes:
            if asyn:
                t.b.dma_writers.append(o)
            else:
                t.b.writers[eng] = o
        o.deps.discard(o)
        self.ops.append(o)
        if asyn:
            self.pending_dma.append(o)
        else:
            self.last[eng] = o
        return o

    def barrier(self):
        deps = list(self.last.values()) + list(self.pending_dma)
        self.pending_dma = []
        for e in ENGS:
            self.op(e, None, after=deps)

    def emit(self):
        nc = self.nc
        per = {e: [] for e in ENGS}
        for o in self.ops:
            per[o.eng].append(o)
        for o in self.ops:
            for d in o.deps:
                if d.eng == o.eng and o.eng == "pe" and not d.is_dma and not o.is_dma:
                    continue
                d.signals = True
        st = self.stack
        sems = {e: st.enter_context(nc.semaphore("s_" + e)) for e in COMPUTE}
        dma_sems = {}
        for e in ENGS:
            if any(o.is_dma and not o.cc for o in per[e]):
                dma_sems[e] = [st.enter_context(nc.semaphore("d_%s%d" % (e, i))) for i in range(self.NDMA_SEMS)]
        cc_sem = st.enter_context(nc.semaphore("s_cc")) if any(o.cc for o in self.ops) else None
        ccnt = 0
        for e in ENGS:
            cnt = 0
            dcnt = 0
            for o in per[e]:
                if o.cc:
                    o.sem = cc_sem
                    o.pre = (cc_sem, ccnt) if ccnt > 0 else None
                    ccnt += 1
                    o.val = ccnt
                    o.signals = True
                elif o.is_dma:
                    s = dma_sems[e][dcnt % self.NDMA_SEMS]
                    k = dcnt // self.NDMA_SEMS
                    o.sem = s
                    o.val = 16 * (k + 1)
                    o.pre = (s, 16 * k) if k > 0 else None
                    o.signals = True
                    dcnt += 1
                elif o.signals and o.fn is not None:
                    cnt += 1
                    o.sem = sems[e]
                    o.val = cnt
        engobj = {"pe": "tensor", "act": "scalar", "dve": "vector", "pool": "gpsimd", "sp": "sync"}

        def run(e, eo):
            waited = {}
            for o in per[e]:
                need = {}
                for d in o.deps:
                    if d.sem is None:
                        continue
                    if d.eng == e and e == "pe" and not d.is_dma and not o.is_dma:
                        continue
                    key = id(d.sem)
                    if key not in need or need[key][1] < d.val:
                        need[key] = (d.sem, d.val)
                if o.pre is not None:
                    key = id(o.pre[0])
                    if key not in need or need[key][1] < o.pre[1]:
                        need[key] = o.pre
                for key, (s, v) in need.items():
                    if waited.get(key, 0) < v:
                        eo.wait_ge(s, v)
                        waited[key] = v
                if o.fn is None:
                    continue
                ins = o.fn(eo)
                if o.cc:
                    ins.then_inc(o.sem, 1)
                elif o.is_dma:
                    ins.then_inc(o.sem, 16)
                elif o.signals:
                    ins.then_inc(o.sem, 1)

        with nc.Block() as block:
            for e in ENGS:
                if per[e]:
                    getattr(block, engobj[e])(lambda eo, e=e: run(e, eo))


class Rot:
    def __init__(self, tiles):
        self.tiles = tiles
        self.i = 0

    def next(self):
        t = self.tiles[self.i % len(self.tiles)]
        self.i += 1
        return t


class KB:
    ARENA = 51000

    def __init__(self, debug=()):
        self.nc = nc = bass.Bass("TRN2", target_bir_lowering=False)
        self.P = Prog(nc)
        self.debug = set(debug)
        st = self.P.stack
        self.arena = st.enter_context(nc.sbuf_tensor("arena", [128, self.ARENA], F32))
        self.pers = st.enter_context(nc.sbuf_tensor("pers", [128, 2048], F32))
        self.pers_off = 0
        self.off = 0
        self.psum = [T(st.enter_context(nc.psum_tensor("ps%d" % i, [128, 512], F32))[:, :]) for i in range(8)]
        self.ps_i = 0
        self.inputs = {}
        self.outs = []

    def din(self, name, shape):
        ap = self.nc.dram_tensor(name, list(shape), F32, kind="ExternalInput").ap()
        self.inputs[name] = tuple(shape)
        return ap

    def scratch(self, name, shape, dt):
        kind = "ExternalOutput" if name in self.debug else "Internal"
        return self.nc.dram_tensor(name, list(shape), dt, kind=kind).ap()

    def reset(self):
        self.P.barrier()
        self.off = 0

    def alloc(self, shape, dt=F32, persistent=False):
        n = int(np.prod(shape[1:]))
        words = n if dt == F32 else (n + 1) // 2
        if persistent:
            base = self.pers[:, self.pers_off:self.pers_off + words]
            self.pers_off += words
            assert self.pers_off <= 2048
        else:
            base = self.arena[:, self.off:self.off + words]
            self.off += words
            assert self.off <= self.ARENA, "arena overflow %d" % self.off
        if dt != F32:
            base = base.bitcast(dt)[:, 0:n]
        if len(shape) == 3:
            base = base.rearrange("p (a b) -> p a b", b=shape[2])
        elif len(shape) == 4:
            base = base.rearrange("p (a b c) -> p a b c", b=shape[2], c=shape[3])
        if shape[0] < 128:
            base = base[0:shape[0]]
        return T(base)

    def next_ps(self, lo=0, hi=8):
        i = lo + self.ps_i % (hi - lo)
        self.ps_i += 1
        return self.psum[i]

    @staticmethod
    def _rd(*xs):
        return [x for x in xs if isinstance(x, T)]

    @staticmethod
    def _a(x):
        return x.ap if isinstance(x, T) else x

    def dma(self, out, in_, q="sp", reads=(), writes=()):
        oa, ia = self._a(out), self._a(in_)
        return self.P.op(q, lambda e: e.dma_start(out=oa, in_=ia), reads=self._rd(in_) + list(reads),
                         writes=self._rd(out) + list(writes), dma=True)

    def mm(self, ps, pairs, first=True, last=True, extra_reads=()):
        n = len(pairs)
        for i, (l, r) in enumerate(pairs):
            la, ra, pa = l.ap, r.ap, ps.ap
            self.P.op("pe", lambda e, la=la, ra=ra, pa=pa, s=(first and i == 0), t=(last and i == n - 1):
                      e.matmul(pa, la, ra, start=s, stop=t), reads=[l, r], writes=[ps])

    def act(self, out, in_, func, bias=0.0, scale=1.0, eng="act"):
        oa, ia, ba, sa = out.ap, in_.ap, self._a(bias), self._a(scale)
        self.P.op("act", lambda e: e.activation(out=oa, in_=ia, func=func, bias=ba, scale=sa),
                  reads=self._rd(in_, bias, scale), writes=[out])

    def ts(self, out, in0, s1, s2, op0, op1=None, eng="dve"):
        oa, ia, a1, a2 = out.ap, in0.ap, self._a(s1), self._a(s2)
        if op1 is None:
            fn = lambda e: e.tensor_scalar(out=oa, in0=ia, scalar1=a1, scalar2=None, op0=op0)
        else:
            fn = lambda e: e.tensor_scalar(out=oa, in0=ia, scalar1=a1, scalar2=a2, op0=op0, op1=op1)
        self.P.op(eng, fn, reads=self._rd(in0, s1, s2), writes=[out])

    def stt(self, out, in0, s, in1, op0, op1, eng="dve"):
        oa, ia, sa, ib = out.ap, in0.ap, self._a(s), in1.ap
        self.P.op(eng, lambda e: e.scalar_tensor_tensor(out=oa, in0=ia, scalar=sa, in1=ib, op0=op0, op1=op1),
                  reads=self._rd(in0, s, in1), writes=[out])

    def tt(self, out, in0, in1, op, eng="dve"):
        oa, ia, ib = out.ap, in0.ap, in1.ap
        self.P.op(eng, lambda e: e.tensor_tensor(out=oa, in0=ia, in1=ib, op=op), reads=[in0, in1], writes=[out])

    def copy(self, out, in_, eng="dve"):
        oa, ia = out.ap, in_.ap
        if eng == "act":
            self.P.op("act", lambda e: e.copy(out=oa, in_=ia), reads=[in_], writes=[out])
        else:
            self.P.op(eng, lambda e: e.tensor_copy(out=oa, in_=ia), reads=[in_], writes=[out])

    def memset(self, out, v, eng="pool"):
        oa = out.ap
        self.P.op(eng, lambda e: e.memset(oa, v), writes=[out])

    def recip(self, out, in_):
        oa, ia = out.ap, in_.ap
        self.P.op("dve", lambda e: e.reciprocal(out=oa, in_=ia), reads=[in_], writes=[out])

    def load_cast(self, dst, src_ap, n, stage, q="sp", eng="pool", CH=2048):
        for c0 in range(0, n, CH):
            c1 = min(n, c0 + CH)
            sg = stage.next()
            self.dma(sg[:, 0:c1 - c0], src_ap[:, c0:c1], q=q)
            self.copy(dst[:, c0:c1], sg[:, 0:c1 - c0], eng=eng)


def flat(t):
    return T(t.ap.rearrange("p a b -> p (a b)"), t.b)


def build(debug=(), stop=None, skip=(), sel_units=None):
    kb = KB(debug)
    nc, P = kb.nc, kb.P
    xT = kb.din("xT", [128, 8, S])
    xown = kb.din("xown", [128, 8, OWN]) if PS > 1 else xT
    c8 = kb.din("c8", [128, 8])
    adaw = kb.din("adaw", [96, 128, 8 * 128])
    adab = kb.din("adab", [128, 96])
    lng = kb.din("lng", [128, 32])
    lnb = kb.din("lnb", [128, 32])
    rope = kb.din("rope", [128, 2, S])
    masks = kb.din("masks", [128, 3, 128])
    ncq = 256 + 128 + 96 + 96 + 3 * HB * 128
    w0in = kb.din("w0in", [128, 8 * ncq])
    w0uq = kb.din("w0uq", [128, 2 * 2 * HA * 96])
    w0ukv = kb.din("w0ukv", [128, 2 * HA * 64])
    vec0 = kb.din("vec0", [128, 4])
    lam4 = kb.din("lam4", [128, 4 * 64])
    t5b = kb.din("t5b", [HB, 2, 128, 128])
    t5c = kb.din("t5c", [128, HB])
    wout = kb.din("wout", [2, 128, 8 * 1024])
    ncd = 3 * HC * 64 + 3 * HD * 64 + HC
    w1in = kb.din("w1in", [128, 8 * ncd])
    bf = kb.din("bf", [HC, 1])
    relb = kb.din("relb", [HD, 2, 128, 128])
    relc = kb.din("relc", [128, HD])
    wg = kb.din("wg", [2, NFC, 128, 8 * 128])
    wv = kb.din("wv", [2, NFC, 128, 8 * 128])
    wd = kb.din("wd", [2, 128, NFC * 1024])
    convw = kb.din("convw", [128, 2 * 3 * NFC])
    convb = kb.din("convb", [128, 2 * NFC])
    outT = nc.dram_tensor("outT", [128, 8, OWN], F32, kind="ExternalOutput").ap()

    qA = kb.scratch("qA", [HA, 96, S], BF16)
    kA = kb.scratch("kA", [HA, 96, S], BF16)
    vA = kb.scratch("vA", [HA, 128, 64, 64], BF16)
    qB = kb.scratch("qB", [HB, 128, S], BF16)
    kBt = kb.scratch("kB", [HB, 128, S], BF16)
    vB = kb.scratch("vB", [HB, 128, 64, 128], BF16)
    odraw = kb.scratch("odraw", [HB, 2, 128, S], F32)
    oT0 = kb.scratch("oT0", [ROWS0, S], BF16)
    oT1 = kb.scratch("oT1", [ROWS1, S], BF16)
    x1T = kb.scratch("x1T", [128, 8, OWN], F32)
    u1T = kb.scratch("u1T", [128, 8, OWN], BF16)
    qC = kb.scratch("qC", [HC, 70, S], BF16)
    kC = kb.scratch("kC", [HC, 70, S], BF16)
    vC = kb.scratch("vC", [HC, 128, 64, 64], BF16)
    qDt = kb.scratch("qD", [HD, 64, S], BF16)
    kDt = kb.scratch("kD", [HD, 64, S], BF16)
    vDt = kb.scratch("vD", [HD, 128, 64, 64], BF16)
    wgb = kb.scratch("wgb", [2, NFC, 128, 8 * 128], BF16)
    wvb = kb.scratch("wvb", [2, NFC, 128, 8 * 128], BF16)
    zd = [T(kb.scratch("zd%d" % i, [1, GW], F32)) for i in range(4)]
    zrot = Rot(zd)

    mods = kb.alloc([128, 96], persistent=True)
    sc1 = kb.alloc([128, 96], persistent=True)
    lng_t = kb.alloc([128, 32], persistent=True)
    lnb_t = kb.alloc([128, 32], persistent=True)
    ones_f = kb.alloc([128, 128], persistent=True)
    vec0_t = kb.alloc([128, 4], persistent=True)
    t5c_t = kb.alloc([128, HB], persistent=True)
    relc_t = kb.alloc([128, HD], persistent=True)
    lamc = kb.alloc([128, 4], persistent=True)
    convw_t = kb.alloc([128, 2 * 3 * NFC], persistent=True)
    convb_t = kb.alloc([128, 2 * NFC], persistent=True)
    bf_t = kb.alloc([128, 1], persistent=True)
    carry = kb.alloc([128, 1], persistent=True)
    mask_b = kb.alloc([128, 3, 128], BF16, persistent=True)
    ones_b = kb.alloc([128, GW], BF16, persistent=True)
    ghalo = kb.alloc([128, NFC, 2], persistent=True)

    def mcol(m, which, kc):
        c = m * 24 + which * 8 + kc
        src = mods if which == 0 else sc1
        return src[:, c:c + 1]

    for dst, src in ((lng_t, lng), (lnb_t, lnb), (vec0_t, vec0), (t5c_t, t5c), (relc_t, relc),
                     (convw_t, convw), (convb_t, convb)):
        kb.dma(dst, src)
    kb.dma(bf_t[0:HC, :], bf)
    kb.memset(ones_f, 1.0)
    kb.memset(ones_b, 1.0)
    kb.memset(ghalo, 0.0)
    kb.memset(carry, 0.0)
    cond = kb.alloc([128, 8])
    adab_t = kb.alloc([128, 96])
    mstage = kb.alloc([128, 3, 128])
    lam_t = kb.alloc([128, 4, 64])
    lam_p = kb.alloc([128, 2, 64])
    lam_s = kb.alloc([128, 2])
    kb.dma(cond, c8)
    kb.dma(adab_t, adab)
    kb.dma(mstage, masks)
    kb.dma(flat(lam_t), lam4)
    kb.copy(mask_b, mstage, eng="pool")
    kb.act(cond, cond, AF.Silu)
    kb.tt(lam_p[:, 0, :], lam_t[:, 0, :], lam_t[:, 1, :], ALU.mult)
    kb.tt(lam_p[:, 1, :], lam_t[:, 2, :], lam_t[:, 3, :], ALU.mult)
    for i in range(2):
        oa, ia = lam_s[:, i:i + 1].ap, lam_p[:, i, :].ap
        P.op("dve", lambda e, oa=oa, ia=ia: e.reduce_sum(out=oa, in_=ia, axis=mybir.AxisListType.X),
             reads=[lam_p], writes=[lam_s])
    kb.act(lam_s, lam_s, AF.Exp)
    kb.tt(lamc[:, 0:1], lam_s[:, 1:2], lam_s[:, 0:1], ALU.subtract)
    kb.ts(lamc[:, 0:1], lamc[:, 0:1], -LAM_INIT, None, ALU.add)
    kb.ts(lamc[:, 1:2], vec0_t[:, 3:4], 1.0 - LAM_INIT, None, ALU.mult)
    wrot = Rot([kb.alloc([128, 8, 128]) for _ in range(3)])
    psm = kb.next_ps()
    for col in range(96):
        wt = wrot.next()
        kb.dma(flat(wt), adaw[col])
        kb.mm(psm[:, col:col + 1], [(wt[:, ic, :], cond[:, ic:ic + 1]) for ic in range(8)])
    kb.tt(mods, psm[:, 0:96], adab_t, ALU.add)
    kb.ts(sc1, mods, 1.0, None, ALU.add)

    def cast_ffn_weights(layer):
        stg = Rot([kb.alloc([128, 1024]) for _ in range(2)])
        stb = Rot([kb.alloc([128, 1024], BF16) for _ in range(2)])
        for fc in range(NFC):
            for src, dst in ((wg, wgb), (wv, wvb)):
                a = stg.next()
                b_ = stb.next()
                kb.dma(a, src[layer, fc], q="pool")
                kb.copy(b_, a, eng="pool")
                kb.dma(dst[layer, fc], b_, q="pool")

    def attention_phase(units):
        kb.reset()
        kt = [kb.alloc([128, S], BF16) for _ in range(2)]
        vt = [kb.alloc([128, 64, 129], BF16) for _ in range(2)]
        qg = Rot([kb.alloc([128, GW], BF16) for _ in range(3)])
        pt = Rot([kb.alloc([128, GW], BF16) for _ in range(5)])
        tmp = Rot([kb.alloc([128, GW]) for _ in range(3)])
        rz = Rot([kb.alloc([128, GW]) for _ in range(2)])
        bc = Rot([kb.alloc([64, GW]) for _ in range(2)])
        ot = Rot([kb.alloc([64, GW]) for _ in range(4)])
        btiles = [[kb.alloc([128, 128]) for _ in range(2)] for _ in range(2)]
        zacc = [kb.alloc([128, GW]) for _ in range(2)]
        rzf = Rot([kb.alloc([128, GW]) for _ in range(2)])
        ofull = Rot([kb.alloc([128, GW]) for _ in range(2)])
        for v in vt:
            kb.memset(v[:, :, 64:65], 1.0)

        def load_unit(ui):
            u = units[ui]
            R = u["R"]
            k_, v_ = kt[ui % 2], vt[ui % 2]
            for c in range(4):
                kb.dma(k_[0:R, c * 2048:(c + 1) * 2048], u["kT"][:, c * 2048:(c + 1) * 2048])
            if u.get("zsum"):
                kb.dma(v_[:, :, 0:128], u["vfull"])
            else:
                kb.dma(v_[:, :, 0:64], u["vsrc"][0])
            if u["bias"] is not None:
                kb.dma(btiles[ui % 2][0], u["bias"][0])
                kb.dma(btiles[ui % 2][1], u["bias"][1])

        def load_q(ui, G):
            u = units[ui]
            t = qg.next()
            kb.dma(t[0:u["R"], :], u["qT"][:, G * GW:(G + 1) * GW])
            return t

        seq = [(ui, G) for ui in range(len(units)) for G in range(NG)]

        def make_plan(kind, G):
            plan = []
            if kind in ("cc", "fc"):
                for kbi in range(4 * G):
                    plan.append((kbi, 0, GW, None))
                for j in range(4):
                    plan.append((4 * G + j, 128 * j, GW, (128 * j, 0 if kind == "cc" else 1)))
            else:
                for r in (-1, -2, -3, -4):
                    if 4 * G + r >= 0:
                        c1 = 64 * (2 * r + 10)
                        plan.append((4 * G + r, 0, c1, (c1 - 128, 2)))
                for r in range(4):
                    plan.append((4 * G + r, 128 * r, GW, (128 * r, 0)))
            return plan

        jobs = []
        for si, (ui, G) in enumerate(seq):
            plan = make_plan(units[ui]["kind"], G)
            for pi, it in enumerate(plan):
                jobs.append((si, ui, G, pi, len(plan), it))
        load_unit(0)
        qn = {0: load_q(0, 0)}
        ptile = {}
        LA = 2

        def front(j):
            si, ui, G, pi, n, (kbi, c0, c1, mk) = jobs[j]
            u = units[ui]
            R, scale = u["R"], u["scale"]
            k_ = kt[ui % 2]
            if pi == 0 and si + 1 < len(seq):
                qn[si + 1] = load_q(*seq[si + 1])
            qcur = qn[si]
            s = kb.next_ps(0, 4)
            kb.mm(s[:, c0:c1], [(k_[0:R, kbi * 128:(kbi + 1) * 128], qcur[0:R, c0:c1])])
            p = pt.next()
            ptile[j] = p
            if u["bias"] is not None:
                bt = btiles[ui % 2]
                cb = u["bias"][2]
                runs = []
                cur = None
                for cc0 in range(c0, c1, 128):
                    dlt = (4 * G + cc0 // 128) - kbi
                    if dlt in (0, 1):
                        tm = tmp.next()
                        kb.stt(tm[:, cc0:cc0 + 128], s[:, cc0:cc0 + 128], scale, bt[dlt], ALU.mult, ALU.add)
                        kb.act(p[:, cc0:cc0 + 128], tm[:, cc0:cc0 + 128], AF.Exp)
                        cur = None
                    else:
                        if cur is None:
                            cur = [cc0, cc0 + 128]
                            runs.append(cur)
                        else:
                            cur[1] = cc0 + 128
                for a0, a1 in runs:
                    kb.act(p[:, a0:a1], s[:, a0:a1], AF.Exp, bias=cb, scale=scale)
            else:
                kb.act(p[:, c0:c1], s[:, c0:c1], AF.Exp, scale=scale)
            if mk is not None:
                m0, mi = mk
                kb.tt(p[:, m0:m0 + 128], p[:, m0:m0 + 128], mask_b[:, mi, :], ALU.mult, eng="pool")
            if u.get("zsum"):
                za = zacc[si % 2]
                if pi == 0:
                    kb.copy(za[:, c0:c1], p[:, c0:c1], eng="dve")
                else:
                    kb.tt(za[:, c0:c1], za[:, c0:c1], p[:, c0:c1], ALU.add, eng="dve")

        def back(j):
            si, ui, G, pi, n, (kbi, c0, c1, mk) = jobs[j]
            u = units[ui]
            nv = len(u["vsrc"])
            v_ = vt[ui % 2]
            if pi == 0 and G == 0 and ui + 1 < len(units):
                load_unit(ui + 1)
            p = ptile.pop(j)
            if u.get("zsum"):
                acc0, zps = kb.psum[4 + 2 * (si % 2)], kb.psum[5 + 2 * (si % 2)]
                kb.mm(acc0[:, c0:c1], [(v_[:, kbi, 0:128], p[:, c0:c1])], first=(pi == 0), last=(pi == n - 1))
                if pi < n - 1:
                    return
                kb.mm(zps, [(ones_f, zacc[si % 2])])
                rz_ = rzf.next()
                kb.recip(rz_, zps)
                o_ = ofull.next()
                kb.tt(o_, acc0, rz_, ALU.mult)
                kb.dma(u["outs"][0][:, G * GW:(G + 1) * GW], o_)
                return
            accs = [kb.psum[4 + 2 * (si % 2) + v] for v in range(nv)]
            for v in range(nv):
                vc0, vc1 = (0, 65) if v == 0 else (65, 129)
                kb.mm(accs[v][0:vc1 - vc0, c0:c1], [(v_[:, kbi, vc0:vc1], p[:, c0:c1])],
                      first=(pi == 0), last=(pi == n - 1))
            if pi < n - 1:
                return
            rzt = rz.next()
            kb.recip(rzt[64:65, :], accs[0][64:65, :])
            zslot = zrot.next()
            kb.dma(zslot, rzt[64:65, :])
            bct = bc.next()
            kb.dma(bct, T(zslot.ap.broadcast_to([64, GW]), zslot.b))
            for v in range(nv):
                o_ = ot.next()
                if u["odt"] == BF16:
                    ob = T(o_.ap.bitcast(BF16)[:, 0:GW], o_.b)
                else:
                    ob = o_
                kb.tt(ob, accs[v][0:64, :], bct, ALU.mult)
                kb.dma(u["outs"][v][:, G * GW:(G + 1) * GW], ob)

        for t in range(len(jobs) + LA):
            if t < len(jobs):
                front(t)
            if t - LA >= 0:
                back(t - LA)

    def phase_a0():
        kb.reset()
        cast_stage = Rot([kb.alloc([128, 2048]) for _ in range(2)])
        W = kb.alloc([128, 8, ncq], BF16)
        Wuq = kb.alloc([128, 2, 2 * HA * 96], BF16)
        Wukv = kb.alloc([128, 2 * HA * 64], BF16)
        kb.load_cast(flat(W), w0in, 8 * ncq, cast_stage)
        kb.load_cast(flat(Wuq), w0uq, 2 * 2 * HA * 96, cast_stage)
        kb.load_cast(Wukv, w0ukv, 2 * HA * 64, cast_stage)
        o_cq, o_ckv, o_kr, o_krs = 0, 256, 384, 480
        o_dq, o_dk, o_dv = 576, 576 + HB * 128, 576 + 2 * HB * 128
        xg = Rot([kb.alloc([128, 8, GW]) for _ in range(2)])
        rp = Rot([kb.alloc([128, 2, GW]) for _ in range(2)])
        ug = Rot([kb.alloc([128, 8, GW], BF16) for _ in range(2)])
        cq_sb = kb.alloc([128, 2, GW])
        sq = Rot([kb.alloc([128, GW]) for _ in range(2)])
        rs = Rot([kb.alloc([128, GW]) for _ in range(2)])
        cqn = kb.alloc([128, 2, GW], BF16)
        ckv_sb = kb.alloc([128, GW])
        ckvn = kb.alloc([128, GW], BF16)
        qh = Rot([kb.alloc([128, GW], BF16) for _ in range(3)])
        t1 = Rot([kb.alloc([128, GW]) for _ in range(2)])
        t2 = Rot([kb.alloc([128, GW]) for _ in range(2)])
        krope = kb.alloc([128, GW], BF16)
        kn = Rot([kb.alloc([64, GW], BF16) for _ in range(3)])
        v_sb = kb.alloc([128, 4, HA * 64], BF16)
        dqk = Rot([kb.alloc([128, GW], BF16) for _ in range(3)])
        vd_sb = kb.alloc([128, 4, HB * 128], BF16)

        def loads(g):
            x_ = xg.next()
            r_ = rp.next()
            kb.dma(x_, xT[:, :, g * GW:(g + 1) * GW])
            kb.dma(r_[64:96], rope[64:96, :, g * GW:(g + 1) * GW])
            return x_, r_

        nxt = loads(0)
        for g in range(NG):
            x_, r_ = nxt
            if g + 1 < NG:
                nxt = loads(g + 1)
            cs = slice(g * GW, (g + 1) * GW)
            u_ = ug.next()
            for kc in range(8):
                kb.ts(u_[:, kc, :], x_[:, kc, :], mcol(0, 1, kc), mcol(0, 0, kc), ALU.mult, ALU.add)
            pss = kb.next_ps()
            for mc in range(2):
                ps = kb.next_ps()
                kb.mm(ps, [(W[:, kc, o_cq + mc * 128:o_cq + (mc + 1) * 128], u_[:, kc, :]) for kc in range(8)])
                kb.copy(cq_sb[:, mc, :], ps, eng="act")
                s_ = sq.next()
                kb.act(s_, ps, AF.Square)
                kb.mm(pss, [(ones_f, s_)], first=(mc == 0), last=(mc == 1))
            r1 = rs.next()
            kb.act(r1, pss, AF.Sqrt, bias=EPS_RMS, scale=1.0 / 256)
            kb.recip(r1, r1)
            for mc in range(2):
                kb.stt(cqn[:, mc, :], cq_sb[:, mc, :], vec0_t[:, mc:mc + 1], r1, ALU.mult, ALU.mult)
            ps = kb.next_ps()
            kb.mm(ps, [(W[:, kc, o_ckv:o_ckv + 128], u_[:, kc, :]) for kc in range(8)])
            kb.copy(ckv_sb, ps, eng="act")
            s_ = sq.next()
            kb.act(s_, ps, AF.Square)
            pss = kb.next_ps()
            kb.mm(pss, [(ones_f, s_)])
            r2 = rs.next()
            kb.act(r2, pss, AF.Sqrt, bias=EPS_RMS, scale=1.0 / 128)
            kb.recip(r2, r2)
            kb.stt(ckvn, ckv_sb, vec0_t[:, 2:3], r2, ALU.mult, ALU.mult)
            for h in range(HA):
                ps1 = kb.next_ps()
                ps2 = kb.next_ps()
                kb.mm(ps1[0:96, :], [(Wuq[:, k2, h * 96:(h + 1) * 96], cqn[:, k2, :]) for k2 in range(2)])
                kb.mm(ps2[0:96, :], [(Wuq[:, k2, HA * 96 + h * 96:HA * 96 + (h + 1) * 96], cqn[:, k2, :]) for k2 in range(2)])
                q_ = qh.next()
                a_, b_ = t1.next(), t2.next()
                kb.copy(q_[0:64, :], ps1[0:64, :], eng="act")
                kb.tt(a_[64:96, :], ps1[64:96, :], r_[64:96, 0, :], ALU.mult)
                kb.tt(b_[64:96, :], ps2[64:96, :], r_[64:96, 1, :], ALU.mult)
                kb.tt(q_[64:96, :], a_[64:96, :], b_[64:96, :], ALU.add)
                kb.dma(qA[h][:, cs], q_[0:96, :])
            ps1 = kb.next_ps()
            ps2 = kb.next_ps()
            kb.mm(ps1[0:96, :], [(W[:, kc, o_kr:o_kr + 96], u_[:, kc, :]) for kc in range(8)])
            kb.mm(ps2[0:96, :], [(W[:, kc, o_krs:o_krs + 96], u_[:, kc, :]) for kc in range(8)])
            a_, b_ = t1.next(), t2.next()
            kb.tt(a_[64:96, :], ps1[64:96, :], r_[64:96, 0, :], ALU.mult)
            kb.tt(b_[64:96, :], ps2[64:96, :], r_[64:96, 1, :], ALU.mult)
            kb.tt(krope[64:96, :], a_[64:96, :], b_[64:96, :], ALU.add)
            for h in range(HA):
                kb.dma(kA[h][64:96, cs], krope[64:96, :])
            for h in range(HA):
                ps = kb.next_ps()
                kb.mm(ps[0:64, :], [(Wukv[:, h * 64:(h + 1) * 64], ckvn)])
                k_ = kn.next()
                kb.copy(k_, ps[0:64, :], eng="act")
                kb.dma(kA[h][0:64, cs], k_)
            for blk in range(4):
                ps = kb.next_ps()
                kb.mm(ps[:, 0:HA * 64], [(ckvn[:, blk * 128:(blk + 1) * 128], Wukv[:, HA * 64:2 * HA * 64])])
                kb.copy(v_sb[:, blk, :], ps[:, 0:HA * 64], eng="dve")
            for h in range(HA):
                kb.dma(vA[h][:, 4 * g:4 * g + 4, :], v_sb[:, :, h * 64:(h + 1) * 64])
            for h in range(HB):
                for off, dst in ((o_dq, qB), (o_dk, kBt)):
                    ps = kb.next_ps()
                    kb.mm(ps, [(W[:, kc, off + h * 128:off + (h + 1) * 128], u_[:, kc, :]) for kc in range(8)])
                    t_ = dqk.next()
                    kb.copy(t_, ps, eng="act")
                    kb.dma(dst[h][:, cs], t_)
            for blk in range(4):
                ps = kb.next_ps()
                kb.mm(ps[:, 0:HB * 128], [(u_[:, kc, blk * 128:(blk + 1) * 128], W[:, kc, o_dv:o_dv + HB * 128])
                                          for kc in range(8)])
                kb.copy(vd_sb[:, blk, :], ps[:, 0:HB * 128], eng="dve")
            for h in range(HB):
                kb.dma(vB[h][:, 4 * g:4 * g + 4, :], vd_sb[:, :, h * 128:(h + 1) * 128])

    def phase_c0():
        kb.reset()
        o0 = Rot([kb.alloc([128, GW]) for _ in range(2)])
        o1 = Rot([kb.alloc([128, GW]) for _ in range(2)])
        dd = Rot([kb.alloc([128, GW]) for _ in range(2)])
        sq = Rot([kb.alloc([128, GW]) for _ in range(2)])
        rs = Rot([kb.alloc([128, GW]) for _ in range(2)])
        ob = Rot([kb.alloc([128, GW], BF16) for _ in range(2)])
        for h in range(HB):
            for g in range(NG):
                cs = slice(g * GW, (g + 1) * GW)
                a, b_ = o0.next(), o1.next()
                kb.dma(a, odraw[h, 0][:, cs])
                kb.dma(b_, odraw[h, 1][:, cs])
                d_ = dd.next()
                kb.stt(d_, b_, lamc[:, 0:1], a, ALU.mult, ALU.add)
                s_ = sq.next()
                kb.act(s_, d_, AF.Square)
                ps = kb.next_ps()
                kb.mm(ps, [(ones_f, s_)])
                r_ = rs.next()
                kb.act(r_, ps, AF.Sqrt, bias=EPS_RMS, scale=1.0 / 128)
                kb.recip(r_, r_)
                o_ = ob.next()
                kb.stt(o_, d_, lamc[:, 1:2], r_, ALU.mult, ALU.mult)
                kb.dma(oT0[HA * 64 + h * 128:HA * 64 + (h + 1) * 128, cs], o_)

    def layer_norm(z, m_idx, xout, extra):
        sqr = Rot(extra["sq"])
        psm, psq = kb.next_ps(), kb.next_ps()
        for kc in range(8):
            s_ = sqr.next()
            kb.act(s_, z[:, kc, :], AF.Square)
            kb.mm(psm, [(ones_f, z[:, kc, :])], first=(kc == 0), last=(kc == 7))
            kb.mm(psq, [(ones_f, s_)], first=(kc == 0), last=(kc == 7))
        mean, msq, rstd = extra["st"]
        kb.ts(mean, psm, 1.0 / D, None, ALU.mult)
        kb.tt(msq, mean, mean, ALU.mult)
        kb.stt(rstd, psq, 1.0 / D, msq, ALU.mult, ALU.subtract)
        kb.act(rstd, rstd, AF.Sqrt, bias=EPS_LN)
        kb.recip(rstd, rstd)
        for kc in range(8):
            kb.tt(z[:, kc, :], z[:, kc, :], mean, ALU.subtract)
            kb.tt(z[:, kc, :], z[:, kc, :], rstd, ALU.mult)
            kb.ts(xout[:, kc, :], z[:, kc, :], lng_t[:, m_idx * 8 + kc:m_idx * 8 + kc + 1],
                  lnb_t[:, m_idx * 8 + kc:m_idx * 8 + kc + 1], ALU.mult, ALU.add)

    def phase_b(layer, o_all, rows):
        kb.reset()
        m_mix, m_ffn = 2 * layer, 2 * layer + 1
        cast_stage = Rot([kb.alloc([128, 1024]) for _ in range(2)])
        Wo = kb.alloc([128, 8, 1024], BF16)
        Wd = kb.alloc([128, NFC, 1024], BF16)
        kb.load_cast(flat(Wo), wout[layer], 8 * 1024, cast_stage, CH=1024)
        kb.load_cast(flat(Wd), wd[layer], NFC * 1024, cast_stage, CH=1024)
        wgv = Rot([kb.alloc([128, 2, 8, 128], BF16) for _ in range(3)])
        og = Rot([kb.alloc([128, 8, GW], BF16) for _ in range(2)])
        xr = Rot([kb.alloc([128, 8, GW]) for _ in range(1)])
        z = kb.alloc([128, 8, GW])
        xn = kb.alloc([128, 8, GW])
        u2 = kb.alloc([128, 8, GW], BF16)
        hT = kb.alloc([128, NFC, GW], BF16)
        lnx = {"sq": [kb.alloc([128, GW]) for _ in range(2)], "st": [kb.alloc([128, GW]) for _ in range(3)]}
        gsb = Rot([kb.alloc([128, GW + 2]) for _ in range(2)])
        acc = Rot([kb.alloc([128, GW]) for _ in range(2)])
        sg = Rot([kb.alloc([128, GW]) for _ in range(2)])
        xsrc = xown if layer == 0 else x1T
        kb.memset(ghalo, 0.0)

        def loads(g):
            o_, x_ = og.next(), xr.next()
            cs = slice(g * GW, (g + 1) * GW)
            for kc in range(8):
                kb.dma(o_[:, kc, :], o_all[kc * 128:(kc + 1) * 128, cs])
            kb.dma(x_, xsrc[:, :, cs])
            return o_, x_

        nxt = loads(0)
        for g in range(NGO):
            o_, x_ = nxt
            cs = slice(g * GW, (g + 1) * GW)
            kb.act(flat(x_), flat(x_), AF.Identity, scale=ALPHA)
            for mc in range(8):
                ps = kb.next_ps()
                kb.mm(ps, [(Wo[:, kc, mc * 128:(mc + 1) * 128], o_[:, kc, :]) for kc in range(8)])
                kb.stt(z[:, mc, :], ps, mcol(m_mix, 2, mc), x_[:, mc, :], ALU.mult, ALU.add)
            if g + 1 < NGO:
                nxt = loads(g + 1)
            layer_norm(z, m_mix, xn, lnx)
            for kc in range(8):
                kb.act(u2[:, kc, :], xn[:, kc, :], AF.Identity, bias=mcol(m_ffn, 0, kc), scale=mcol(m_ffn, 1, kc))
            kb.act(flat(xn), flat(xn), AF.Identity, scale=ALPHA)
            for fc in range(NFC):
                w_ = wgv.next()
                kb.dma(flat(w_[:, 0]), wgb[layer, fc])
                kb.dma(flat(w_[:, 1]), wvb[layer, fc])
                psg, psv = kb.next_ps(), kb.next_ps()
                kb.mm(psg, [(w_[:, 0, kc, :], u2[:, kc, :]) for kc in range(8)])
                kb.mm(psv, [(w_[:, 1, kc, :], u2[:, kc, :]) for kc in range(8)])
                g_ = gsb.next()
                kb.copy(g_[:, 0:2], ghalo[:, fc, :], eng="pool")
                kb.copy(g_[:, 2:GW + 2], psg, eng="act")
                kb.copy(ghalo[:, fc, :], g_[:, GW:GW + 2], eng="pool")
                a_ = acc.next()
                cw = lambda j: convw_t[:, (layer * 3 + j) * NFC + fc:(layer * 3 + j) * NFC + fc + 1]
                kb.ts(a_, g_[:, 0:GW], cw(0), convb_t[:, layer * NFC + fc:layer * NFC + fc + 1], ALU.mult, ALU.add)
                kb.stt(a_, g_[:, 1:GW + 1], cw(1), a_, ALU.mult, ALU.add)
                kb.stt(a_, g_[:, 2:GW + 2], cw(2), a_, ALU.mult, ALU.add)
                s_ = sg.next()
                kb.act(s_, a_, AF.Silu)
                kb.tt(hT[:, fc, :], s_, psv, ALU.mult)
            for mc in range(8):
                ps = kb.next_ps()
                kb.mm(ps, [(Wd[:, fc, mc * 128:(mc + 1) * 128], hT[:, fc, :]) for fc in range(NFC)])
                kb.stt(z[:, mc, :], ps, mcol(m_ffn, 2, mc), xn[:, mc, :], ALU.mult, ALU.add)
            layer_norm(z, m_ffn, xn, lnx)
            if layer == 0:
                kb.dma(x1T[:, :, cs], xn)
                u_ = u2
                for kc in range(8):
                    kb.act(u_[:, kc, :], xn[:, kc, :], AF.Identity, bias=mcol(2, 0, kc), scale=mcol(2, 1, kc))
                kb.dma(u1T[:, :, cs], u_)
            else:
                kb.dma(outT[:, :, cs], xn)

    def phase_a1(u_all):
        kb.reset()
        cast_stage = Rot([kb.alloc([128, 2048]) for _ in range(2)])
        W = kb.alloc([128, 8, ncd], BF16)
        kb.load_cast(flat(W), w1in, 8 * ncd, cast_stage)
        o_cq, o_ck, o_cv = 0, HC * 64, 2 * HC * 64
        o_dq, o_dk, o_dv = 3 * HC * 64, 3 * HC * 64 + HD * 64, 3 * HC * 64 + 2 * HD * 64
        o_fl = 3 * HC * 64 + 3 * HD * 64
        ug = Rot([kb.alloc([128, 8, GW], BF16) for _ in range(2)])
        qk = Rot([kb.alloc([128, GW], BF16) for _ in range(3)])
        vc_sb = kb.alloc([128, 4, HC * 64], BF16)
        vd_sb = kb.alloc([128, 4, HD * 64], BF16)
        e_ = kb.alloc([128, GW])
        csA = kb.alloc([128, GW])
        csB = kb.alloc([128, GW])
        f8 = kb.alloc([128, GW])
        r1 = kb.alloc([128, GW])
        fp_ = Rot([kb.alloc([128, 6, GW], BF16) for _ in range(2)])
        kb.memset(carry, 0.0)

        def loads(g):
            u_ = ug.next()
            kb.dma(u_, u_all[:, :, g * GW:(g + 1) * GW])
            return u_

        nxt = loads(0)
        for g in range(NG):
            u_ = nxt
            if g + 1 < NG:
                nxt = loads(g + 1)
            cs = slice(g * GW, (g + 1) * GW)
            for off, dst, nh in ((o_cq, qC, HC), (o_ck, kC, HC), (o_dq, qDt, HD), (o_dk, kDt, HD)):
                for hp in range(nh // 2):
                    ps = kb.next_ps()
                    kb.mm(ps, [(W[:, kc, off + hp * 128:off + (hp + 1) * 128], u_[:, kc, :]) for kc in range(8)])
                    t_ = qk.next()
                    kb.copy(t_, ps, eng="act")
                    kb.dma(dst[2 * hp][0:64, cs], t_[0:64, :])
                    kb.dma(dst[2 * hp + 1][0:64, cs], t_[64:128, :])
            for off, sb, dst, nh in ((o_cv, vc_sb, vC, HC), (o_dv, vd_sb, vDt, HD)):
                for blk in range(4):
                    ps = kb.next_ps()
                    kb.mm(ps[:, 0:nh * 64], [(u_[:, kc, blk * 128:(blk + 1) * 128], W[:, kc, off:off + nh * 64])
                                             for kc in range(8)])
                    kb.copy(sb[:, blk, :], ps[:, 0:nh * 64], eng="dve")
                for h in range(nh):
                    kb.dma(dst[h][:, 4 * g:4 * g + 4, :], sb[:, :, h * 64:(h + 1) * 64])
            ps = kb.next_ps()
            kb.mm(ps[0:HC, :], [(W[:, kc, o_fl:o_fl + HC], u_[:, kc, :]) for kc in range(8)])
            kb.ts(e_[0:HC, :], ps[0:HC, :], bf_t[0:HC, :], -1.0, ALU.add, ALU.mult)
            kb.act(e_[0:HC, :], e_[0:HC, :], AF.Exp)
            kb.act(csA[0:HC, :], e_[0:HC, :], AF.Ln, bias=1.0)
            a, b_ = csA, csB
            sft = 1
            while sft < GW:
                kb.copy(b_[0:HC, 0:sft], a[0:HC, 0:sft], eng="dve")
                kb.tt(b_[0:HC, sft:GW], a[0:HC, sft:GW], a[0:HC, 0:GW - sft], ALU.add)
                a, b_ = b_, a
                sft *= 2
            kb.ts(f8[0:HC, :], a[0:HC, :], carry[0:HC, :], -8.0, ALU.add, ALU.mult)
            kb.tt(carry[0:HC, :], carry[0:HC, :], a[0:HC, GW - 1:GW], ALU.add)
            fp = fp_.next()
            kb.copy(fp[0:HC, 0, :], f8[0:HC, :], eng="dve")
            kb.tt(r1[0:HC, :], f8[0:HC, :], fp[0:HC, 0, :], ALU.subtract)
            kb.copy(fp[0:HC, 1, :], r1[0:HC, :], eng="dve")
            kb.tt(r1[0:HC, :], r1[0:HC, :], fp[0:HC, 1, :], ALU.subtract)
            kb.copy(fp[0:HC, 2, :], r1[0:HC, :], eng="dve")
            for j in range(3):
                kb.ts(fp[0:HC, 3 + j, :], fp[0:HC, j, :], -1.0, None, ALU.mult)
            for h in range(HC):
                kb.dma(qC[h:h + 1, 64:67, cs], fp[h:h + 1, 0:3, :])
                kb.dma(kC[h:h + 1, 67:70, cs], fp[h:h + 1, 3:6, :])
                kb.dma(qC[h][67:70, cs], ones_b[0:3, :])
                kb.dma(kC[h][64:67, cs], ones_b[0:3, :])

    t5_tiles = [(T(t5b[h, 0]), T(t5b[h, 1])) for h in range(HB)]
    rel_tiles = [(T(relb[h, 0]), T(relb[h, 1])) for h in range(HD)]
    units0 = []
    for h in range(HA):
        units0.append(dict(qT=qA[h], kT=kA[h], R=96, vsrc=[vA[h]], scale=SCALE_A, kind="cc", bias=None,
                           outs=[oT0[h * 64:(h + 1) * 64, :]], odt=BF16))
    for h in range(HB):
        for c in range(2):
            units0.append(dict(qT=qB[h][64 * c:64 * c + 64, :], kT=kBt[h][64 * c:64 * c + 64, :], R=64,
                               vsrc=[vB[h][:, :, 0:64]], vfull=vB[h], zsum=True, scale=SCALE_B, kind="cc",
                               bias=(t5_tiles[h][0], t5_tiles[h][1], t5c_t[:, h:h + 1]),
                               outs=[odraw[h, c]], odt=F32))
    units1 = []
    for h in range(HC):
        units1.append(dict(qT=qC[h], kT=kC[h], R=70, vsrc=[vC[h]], scale=SCALE_B, kind="fc", bias=None,
                           outs=[oT1[h * 64:(h + 1) * 64, :]], odt=BF16))
    for h in range(HD):
        units1.append(dict(qT=qDt[h], kT=kDt[h], R=64, vsrc=[vDt[h]], scale=SCALE_B, kind="band",
                           bias=(rel_tiles[h][0], rel_tiles[h][1], relc_t[:, h:h + 1]),
                           outs=[oT1[HC * 64 + h * 64:HC * 64 + (h + 1) * 64, :]], odt=BF16))
    if sel_units is not None:
        units0 = [units0[i] for i in sel_units[0]]
        units1 = [units1[i] for i in sel_units[1]]
    steps = [("a0", phase_a0), ("cast", lambda: (cast_ffn_weights(0), cast_ffn_weights(1))),
             ("att0", lambda: attention_phase(units0)), ("c0", phase_c0), ("b0", lambda: phase_b(0, oT0, ROWS0)),
             ("a1", lambda: phase_a1(u1T)), ("att1", lambda: attention_phase(units1)),
             ("b1", lambda: phase_b(1, oT1, ROWS1))]
    for name, fn in steps:
        if skip and name in skip:
            continue
        fn()
        if stop == name:
            break
    kb.reset()
    P.emit()
    P.stack.close()
    return nc, kb


def _t5_bucket(rel):
    half, max_exact = 16, 8
    n = np.abs(rel)
    large = max_exact + (np.log(np.maximum(n, 1).astype(np.float32) / max_exact)
                         / math.log(128 / max_exact) * (half - max_exact)).astype(np.int32)
    large = np.minimum(large, half - 1)
    return np.where(rel > 0, half, 0) + np.where(n < max_exact, n, large)


def _pk(w):
    K, N = w.shape
    return np.ascontiguousarray(w.reshape(K // 128, 128, N).transpose(1, 0, 2).reshape(128, (K // 128) * N))


def _col8(v):
    return np.ascontiguousarray(v.reshape(8, 128).T)


def prep_inputs(inp, core):
    f = np.float32
    b, r = core // PS, core % PS
    x = inp["x"][b]
    m = {}
    xTa = np.ascontiguousarray(x.T.reshape(8, 128, S).transpose(1, 0, 2))
    m["xT"] = xTa
    if PS > 1:
        m["xown"] = np.ascontiguousarray(xTa[:, :, r * OWN:(r + 1) * OWN])
    m["c8"] = _col8(inp["c"][b])
    aw = inp["ada_w"].reshape(4, 8, 128, 24, 128)
    m["adaw"] = np.ascontiguousarray(aw.transpose(0, 3, 2, 1, 4).reshape(96, 128, 8 * 128))
    ab = inp["ada_b"].reshape(4, 24, 128)
    m["adab"] = np.ascontiguousarray(ab.transpose(2, 0, 1).reshape(128, 96))
    m["lng"] = np.ascontiguousarray(inp["ln_g"].reshape(4, 8, 128).transpose(2, 0, 1).reshape(128, 32))
    m["lnb"] = np.ascontiguousarray(inp["ln_b"].reshape(4, 8, 128).transpose(2, 0, 1).reshape(128, 32))
    half = 16
    inv = np.power(np.float32(10000.0), -np.arange(half, dtype=f) / half).astype(f)
    ang = np.arange(S, dtype=f)[:, None] * inv[None, :]
    cos, sin = np.cos(ang).astype(f).T, np.sin(ang).astype(f).T
    rp = np.zeros((128, 2, S), f)
    rp[64:96, 0] = np.concatenate([cos, cos], 0)
    rp[64:96, 1] = np.concatenate([-sin, sin], 0)
    m["rope"] = rp
    k = np.arange(128)[:, None]
    q = np.arange(128)[None, :]
    mk = np.ones((128, 3, 128), f)
    mk[:, 0] = np.where((k >= 64) & (q < 64), 0.0, 1.0)
    mk[:, 1] = np.where(k <= q, 1.0, 0.0)
    mk[:, 2] = np.where((k < 64) & (q >= 64), 0.0, 1.0)
    m["masks"] = mk
    w = inp["ab_w_in"][0]
    ha = list(range(r * HA, (r + 1) * HA))
    hb = list(range(r * HB, (r + 1) * HB))
    z64 = np.zeros((D, 64), f)
    kr = w[:, 384:416]
    cols = [w[:, 0:256], w[:, 256:384], z64, kr, z64, kr[:, 16:32], kr[:, 0:16]]
    for base in (416, 928, 1440):
        cols += [w[:, base + h * 128:base + (h + 1) * 128] for h in hb]
    m["w0in"] = _pk(np.concatenate(cols, 1))
    uq = inp["mla_w_uq"][0]
    a1 = [uq[:, h * 96:(h + 1) * 96] for h in ha]
    a2 = [np.concatenate([uq[:, h * 96:h * 96 + 64], uq[:, h * 96 + 80:h * 96 + 96], uq[:, h * 96 + 64:h * 96 + 80]], 1)
          for h in ha]
    m["w0uq"] = _pk(np.concatenate(a1 + a2, 1))
    ukv = inp["mla_w_ukv"][0]
    m["w0ukv"] = _pk(np.concatenate([ukv[:, h * 128:h * 128 + 64] for h in ha] +
                                    [ukv[:, h * 128 + 64:h * 128 + 128] for h in ha], 1))
    v0 = np.zeros((128, 4), f)
    v0[:, 0] = inp["mla_q_norm"][0, 0:128]
    v0[:, 1] = inp["mla_q_norm"][0, 128:256]
    v0[:, 2] = inp["mla_kv_norm"][0]
    v0[:, 3] = inp["diff_sub_g"][0]
    m["vec0"] = v0
    lam = np.concatenate([inp["diff_lq1"][0], inp["diff_lk1"][0], inp["diff_lq2"][0], inp["diff_lk2"][0]])
    m["lam4"] = np.ascontiguousarray(np.broadcast_to(lam[None, :], (128, 256)))
    t5 = inp["t5_table"]
    bd = _t5_bucket(k - q)
    bp = _t5_bucket((k - 128) - q)
    m["t5b"] = np.ascontiguousarray(np.stack([np.stack([t5[h][bd], t5[h][bp]]) for h in hb]))
    m["t5c"] = np.ascontiguousarray(np.broadcast_to(t5[hb, 15][None, :], (128, HB)))
    wo = inp["ab_w_out"][0]
    rows0 = np.concatenate([wo[h * 64:(h + 1) * 64] for h in range(8)] + [wo[512 + h * 128:512 + (h + 1) * 128] for h in range(4)], 0)
    wo1 = inp["cd_w_out"][0]
    m["wout"] = np.stack([_pk(rows0), _pk(wo1)])
    w = inp["cd_w_in"][0]
    hc = list(range(r * HC, (r + 1) * HC))
    hd = list(range(r * HD, (r + 1) * HD))
    cols = []
    for base in (0, 512, 1024):
        cols += [w[:, base + h * 64:base + (h + 1) * 64] for h in hc]
    for base in (1544, 2056, 2568):
        cols += [w[:, base + h * 64:base + (h + 1) * 64] for h in hd]
    cols += [w[:, 1536 + h:1536 + h + 1] for h in hc]
    m["w1in"] = _pk(np.concatenate(cols, 1))
    m["bf"] = np.ascontiguousarray(inp["fox_b_f"][0][hc][:, None])
    rt = inp["chunk_rel_table"][0]
    idd = np.clip(q - k, -128, 128) + 128
    idp = np.clip(q + 128 - k, -128, 128) + 128
    m["relb"] = np.ascontiguousarray(np.stack([np.stack([rt[h][idd], rt[h][idp]]) for h in hd]))
    m["relc"] = np.ascontiguousarray(np.broadcast_to(rt[hd, 256][None, :], (128, HD)))
    def pk_fc(wm):
        return np.ascontiguousarray(wm.reshape(2, 8, 128, NFC, 128).transpose(0, 3, 2, 1, 4).reshape(2, NFC, 128, 8 * 128))
    m["wg"] = pk_fc(inp["ffn_w_gate"])
    m["wv"] = pk_fc(inp["ffn_w_val"])
    m["wd"] = np.stack([_pk(inp["ffn_w_down"][l]) for l in range(2)])
    cw = inp["ffn_conv_w"].reshape(2, 3, NFC, 128)
    m["convw"] = np.ascontiguousarray(cw.transpose(3, 0, 1, 2).reshape(128, 2 * 3 * NFC))
    m["convb"] = np.ascontiguousarray(inp["ffn_conv_b"].reshape(2, NFC, 128).transpose(2, 0, 1).reshape(128, 2 * NFC))
    return {kk: np.ascontiguousarray(vv, dtype=f) for kk, vv in m.items()}


_CACHE = {}


def kernel(**inputs):
    inp = {k_: np.asarray(v) for k_, v in inputs.items()}
    if "nc" not in _CACHE:
        _CACHE["nc"] = build()
    nc, kb = _CACHE["nc"]
    in_maps = [prep_inputs(inp, c) for c in range(NCORES)]
    res = run_bass_kernel_spmd(nc, in_maps, core_ids=list(range(NCORES)))
    out = np.empty((4, S, D), np.float32)
    for c in range(NCORES):
        b, r = c // PS, c % PS
        o = res.results[c]["outT"]
        out[b, r * OWN:(r + 1) * OWN, :] = o.transpose(2, 1, 0).reshape(OWN, D)
    return out
```
